# Optimizing a Trainium2 kernel written in Bass

```python
import jax, jax.numpy as jnp
from jax import lax
import numpy as np

D_MODEL = 4096
BATCH = 1
SEQ = 8192
DEPTH = 2

N_MIXERS = 2
N_RET_LAYERS = (DEPTH + 1) // N_MIXERS
N_FNO_LAYERS = DEPTH // N_MIXERS
RET_HEADS = 16
RET_QK_DIM = D_MODEL // RET_HEADS
RET_V_DIM = 2 * RET_QK_DIM
RET_V_WIDTH = RET_HEADS * RET_V_DIM
RET_IN_WIDTH = 2 * D_MODEL + 2 * RET_V_WIDTH
RET_CHUNK = 128
ROPE_BASE = 10000.0
FNO_GROUPS = 8
FNO_GROUP_DIM = D_MODEL // FNO_GROUPS
D_FF = 4 * D_MODEL
N_MOD = 6
EPS = 1e-6

kernel_name = "hybrid_retention_fourier_adaln_encoder"


def rms_norm(x, g):
    xf = x.astype(jnp.float32)
    y = xf * lax.rsqrt(jnp.mean(xf * xf, axis=-1, keepdims=True) + EPS)
    return (y * g.astype(jnp.float32)).astype(x.dtype)


def adaln_params(c, w, b):
    m = jax.nn.silu(c) @ w + b
    return jnp.split(m[:, None, :], N_MOD, axis=-1)


def rotary(x, pos):
    half = x.shape[-1] // 2
    inv = ROPE_BASE ** (-jnp.arange(half, dtype=jnp.float32) / half)
    ang = pos[:, None] * inv[None, :]
    cos = jnp.cos(ang)[None, :, None, :]
    sin = jnp.sin(ang)[None, :, None, :]
    x1, x2 = x[..., :half], x[..., half:]
    return jnp.concatenate([x1 * cos - x2 * sin, x1 * sin + x2 * cos], axis=-1)


def retention_chunkwise(q, k, v, log_gamma):
    B, S, H, dk = q.shape
    dv = v.shape[-1]
    C = RET_CHUNK
    n = S // C
    qc = q.reshape(B, n, C, H, dk)
    kc = k.reshape(B, n, C, H, dk)
    vc = v.reshape(B, n, C, H, dv)
    j = jnp.arange(C, dtype=jnp.float32)
    diff = j[:, None] - j[None, :]
    intra = jnp.where(diff[None] >= 0,
                      jnp.exp(jnp.maximum(diff, 0.0)[None] * log_gamma[:, None, None]),
                      0.0)
    scores = jnp.einsum('bnihd,bnjhd->bnhij', qc, kc) * intra[None, None]
    out_intra = jnp.einsum('bnhij,bnjhe->bnihe', scores, vc)
    q_decay = jnp.exp((j + 1.0)[:, None] * log_gamma[None, :])
    k_decay = jnp.exp((C - 1.0 - j)[:, None] * log_gamma[None, :])
    chunk_decay = jnp.exp(C * log_gamma)

    def step(state, xs):
        qi, ki, vi = xs
        out = jnp.einsum('bihd,bhde->bihe', qi * q_decay[None, :, :, None], state)
        state = state * chunk_decay[None, :, None, None] + jnp.einsum(
            'bihd,bihe->bhde', ki * k_decay[None, :, :, None], vi)
        return state, out

    state0 = jnp.zeros((B, H, dk, dv), jnp.float32)
    xs = (qc.transpose(1, 0, 2, 3, 4), kc.transpose(1, 0, 2, 3, 4), vc.transpose(1, 0, 2, 3, 4))
    _, out_cross = lax.scan(step, state0, xs)
    out = out_intra + out_cross.transpose(1, 0, 2, 3, 4)
    return out.reshape(B, S, H, dv)


def retention_mixer(h, w_in, w_out, gn_g, dec_fwd, dec_bwd):
    B, S, _ = h.shape
    proj = h @ w_in
    q, k, v, g = jnp.split(proj, [D_MODEL, 2 * D_MODEL, 2 * D_MODEL + RET_V_WIDTH], axis=-1)
    pos = jnp.arange(S, dtype=jnp.float32)
    q = rotary(q.astype(jnp.float32).reshape(B, S, RET_HEADS, RET_QK_DIM), pos)
    k = rotary(k.astype(jnp.float32).reshape(B, S, RET_HEADS, RET_QK_DIM), pos) * (RET_QK_DIM ** -0.5)
    v = v.astype(jnp.float32).reshape(B, S, RET_HEADS, RET_V_DIM)
    lg_f = jax.nn.log_sigmoid(dec_fwd.astype(jnp.float32))
    lg_b = jax.nn.log_sigmoid(dec_bwd.astype(jnp.float32))
    y_f = retention_chunkwise(q, k, v, lg_f)
    y_b = jnp.flip(retention_chunkwise(jnp.flip(q, 1), jnp.flip(k, 1), jnp.flip(v, 1), lg_b), 1)
    y = y_f + y_b
    mu = jnp.mean(y, axis=-1, keepdims=True)
    var = jnp.mean(jnp.square(y - mu), axis=-1, keepdims=True)
    y = ((y - mu) * lax.rsqrt(var + EPS)).reshape(B, S, RET_V_WIDTH) * gn_g.astype(jnp.float32)
    y = jax.nn.silu(g.astype(jnp.float32)) * y
    return y.astype(h.dtype) @ w_out


def fourier_mixer(h, w_f, b_f):
    B, S, D = h.shape
    hg = h.astype(jnp.float32).reshape(B, S, FNO_GROUPS, FNO_GROUP_DIM)
    mixed = jnp.fft.fft2(hg, axes=(1, 3)).real.reshape(B, S, D)
    return mixed.astype(h.dtype) @ w_f + b_f


def sq_relu_mlp(h, w1, w2):
    a = jax.nn.relu(h @ w1)
    return (a * a) @ w2


def setup_inputs(seed: int = 0) -> dict:
    key = jax.random.key(seed)
    ks = jax.random.split(key, 16)
    f32 = jnp.float32
    D = D_MODEL
    base_logit = np.log(2.0 ** (5.0 + np.arange(RET_HEADS)) - 1.0).astype(np.float32)
    return {
        "x": jax.random.normal(ks[0], (BATCH, SEQ, D), f32),
        "c": jax.random.normal(ks[1], (BATCH, D), f32),
        "ada_w": jax.random.normal(ks[2], (DEPTH, D, N_MOD * D), f32) * (0.5 * D ** -0.5),
        "ada_b": jax.random.normal(ks[3], (DEPTH, N_MOD * D), f32) * 0.02,
        "norm_g": 1.0 + 0.02 * jax.random.normal(ks[4], (DEPTH, 4, D), f32),
        "ret_w_in": jax.random.normal(ks[5], (N_RET_LAYERS, D, RET_IN_WIDTH), f32) * D ** -0.5,
        "ret_w_out": jax.random.normal(ks[6], (N_RET_LAYERS, RET_V_WIDTH, D), f32) * RET_V_WIDTH ** -0.5,
        "ret_gn_g": 1.0 + 0.02 * jax.random.normal(ks[7], (N_RET_LAYERS, RET_V_WIDTH), f32),
        "ret_decay_fwd": jnp.asarray(base_logit)[None, :] + 0.1 * jax.random.normal(ks[8], (N_RET_LAYERS, RET_HEADS), f32),
        "ret_decay_bwd": jnp.asarray(base_logit[::-1].copy())[None, :] + 0.1 * jax.random.normal(ks[9], (N_RET_LAYERS, RET_HEADS), f32),
        "fno_w": jax.random.normal(ks[10], (N_FNO_LAYERS, D, D), f32) * D ** -0.5,
        "fno_b": jax.random.normal(ks[11], (N_FNO_LAYERS, D), f32) * 0.02,
        "mlp_w1": jax.random.normal(ks[12], (DEPTH, D, D_FF), f32) * D ** -0.5,
        "mlp_w2": jax.random.normal(ks[13], (DEPTH, D_FF, D), f32) * D_FF ** -0.5,
    }


def reference(x, c, ada_w, ada_b, norm_g, ret_w_in, ret_w_out, ret_gn_g,
              ret_decay_fwd, ret_decay_bwd, fno_w, fno_b, mlp_w1, mlp_w2):
    for layer in range(DEPTH):
        occ = layer // N_MIXERS
        sh1, sc1, g1, sh2, sc2, g2 = adaln_params(c, ada_w[layer], ada_b[layer])
        h = rms_norm(x, norm_g[layer, 0]) * (1.0 + sc1) + sh1
        if layer % N_MIXERS == 0:
            y = retention_mixer(h, ret_w_in[occ], ret_w_out[occ], ret_gn_g[occ],
                                ret_decay_fwd[occ], ret_decay_bwd[occ])
        else:
            y = fourier_mixer(h, fno_w[occ], fno_b[occ])
        x = x + g1 * rms_norm(y, norm_g[layer, 1])
        h = rms_norm(x, norm_g[layer, 2]) * (1.0 + sc2) + sh2
        x = x + g2 * rms_norm(sq_relu_mlp(h, mlp_w1[layer], mlp_w2[layer]), norm_g[layer, 3])
    return x
```

```python
import contextlib
import numpy as np
import ml_dtypes
import concourse.bass as bass
import concourse.mybir as mybir
from concourse.bass_utils import run_bass_kernel_spmd

F32 = mybir.dt.float32
BF16 = mybir.dt.bfloat16
AF = mybir.ActivationFunctionType
ALU = mybir.AluOpType
NPBF = ml_dtypes.bfloat16

D = 4096
SEQ = 8192
NCORE = 8
T = SEQ // NCORE
KC = D // 128
HEADS = 16
DK = 256
DV = 512
VW = HEADS * DV
DFF = 4 * D
EPS = 1e-6
ROPE_BASE = 10000.0


class Buf:
    __slots__ = ("name", "w", "r", "sem", "tot")

    def __init__(self, name):
        self.name = name
        self.w = []
        self.r = {}
        self.sem = None
        self.tot = 0


class Ins:
    __slots__ = ("eng", "fn", "deps", "isdma", "sem", "val", "needed")

    def __init__(self, eng, fn, isdma):
        self.eng = eng
        self.fn = fn
        self.isdma = isdma
        self.deps = []
        self.sem = None
        self.val = 0
        self.needed = False


class Prog:
    ENGS = ("pe", "act", "dve", "pool", "sp")

    def __init__(self, nc):
        self.nc = nc
        self.stack = contextlib.ExitStack()
        self.streams = {e: [] for e in self.ENGS}
        self.nsem = 0
        self.prog_sem = {e: self.sem("prog_" + e) for e in self.ENGS}
        self.n_t = 0

    def sem(self, name):
        self.nsem += 1
        return self.stack.enter_context(self.nc.semaphore(name))

    def sbuf(self, name, shape, dt):
        return self.stack.enter_context(self.nc.sbuf_tensor(name, list(shape), dt))

    def psum(self, name, shape, dt):
        return self.stack.enter_context(self.nc.psum_tensor(name, list(shape), dt))

    def bufs(self, name, n):
        return [Buf("%s%d" % (name, i)) for i in range(n)]

    def _track(self, ins, reads, writes):
        deps = {}
        grp = {}
        for b in reads:
            for w in b.w:
                deps[id(w)] = w
        for b in writes:
            g = bool(ins.isdma and b.w and not b.r and all(w.isdma for w in b.w))
            grp[id(b)] = g
            if not g:
                for w in b.w:
                    deps[id(w)] = w
            for r in b.r.values():
                deps[id(r)] = r
        deps.pop(id(ins), None)
        for b in reads:
            b.r[id(ins) if ins.isdma else ins.eng] = ins
        for b in writes:
            if grp[id(b)]:
                b.w.append(ins)
            else:
                b.w = [ins]
                b.r = {}
        for d in deps.values():
            if d.eng == "pe" and ins.eng == "pe" and not d.isdma and not ins.isdma:
                continue
            d.needed = True
            ins.deps.append(d)

    def op(self, eng, fn, reads=(), writes=()):
        ins = Ins(eng, fn, False)
        self._track(ins, reads, writes)
        self.streams[eng].append(ins)
        return ins

    def dma(self, queue, out, in_, reads=(), writes=(), key=None, **kw):
        ins = Ins(queue, lambda e: e.dma_start(out=out, in_=in_, **kw), True)
        if key.sem is None:
            key.sem = self.sem("d_" + key.name)
        key.tot += 16
        ins.sem = key.sem
        ins.val = key.tot
        self._track(ins, reads, writes)
        self.streams[queue].append(ins)
        return ins

    def emit(self, final_waits=()):
        nc = self.nc
        for e in self.ENGS:
            c = 0
            for ins in self.streams[e]:
                if not ins.isdma and ins.needed:
                    c += 1
                    ins.sem = self.prog_sem[e]
                    ins.val = c
        handles = {"pe": "tensor", "act": "scalar", "dve": "vector", "pool": "gpsimd", "sp": "sync"}

        def run(ename, eng):
            known = {}
            for ins in self.streams[ename]:
                need = {}
                for d in ins.deps:
                    k = id(d.sem)
                    if k not in need or need[k][1] < d.val:
                        need[k] = (d.sem, d.val)
                for k, (s, v) in need.items():
                    if known.get(k, 0) >= v:
                        continue
                    eng.wait_ge(s, v)
                    known[k] = v
                r = ins.fn(eng)
                if ins.isdma:
                    r.then_inc(ins.sem, 16)
                elif ins.needed:
                    r.then_inc(ins.sem, 1)
            if ename == "sp":
                for b in final_waits:
                    if b.sem is not None:
                        eng.wait_ge(b.sem, b.tot)

        with nc.Block() as block:
            @block.tensor
            def _(e):
                run("pe", e)

            @block.scalar
            def _(e):
                run("act", e)

            @block.vector
            def _(e):
                run("dve", e)

            @block.gpsimd
            def _(e):
                run("pool", e)

            @block.sync
            def _(e):
                run("sp", e)
        self.stack.close()


class Ring:
    def __init__(self, P, name, n, shape, dt):
        self.t = [P.sbuf("%s%d" % (name, i), shape, dt) for i in range(n)]
        self.b = P.bufs(name, n)
        self.i = 0
        self.n = n

    def next(self):
        k = self.i % self.n
        self.i += 1
        return self.t[k], self.b[k]


class Ctx:
    def __init__(self, P):
        self.P = P
        self.ps = [P.psum("ps%d" % i, [128, 512], F32) for i in range(8)]
        self.psb = P.bufs("psb", 8)
        self.ones = P.sbuf("ones", [128, 128], BF16)
        self.ones_b = Buf("ones")
        P.op("pool", lambda e: e.memset(self.ones[:], 1.0), writes=[self.ones_b])
        self.f32r = Ring(P, "f32r", 4, [128, 1024], F32)
        self.bf1k = Ring(P, "bf1k", 3, [128, 1024], BF16)
        self.st16 = Ring(P, "st16", 6, [128, 512], BF16)
        self.st32 = Ring(P, "st32", 4, [128, 512], F32)
        self.wr = None
        self.atr = None

    def wring(self):
        if self.wr is None:
            self.wr = Ring(self.P, "wsl", 3, [128, 16, 512], BF16)
        return self.wr

    def atring(self):
        if self.atr is None:
            self.atr = Ring(self.P, "atb", 2, [128, 8, T], BF16)
        return self.atr


def stats_rstd(C, src_tiles, nkc, out_rstd, out_b, nfeat, bias_col=None):
    P = C.P
    for kc in range(nkc):
        xt, xb = src_tiles(kc)
        sq, sqb = C.bf1k.next()
        if bias_col is not None:
            bc = bias_col(kc)
            P.op("act", lambda e, sq=sq, xt=xt, bc=bc: e.activation(out=sq[:], in_=xt, func=AF.Square, bias=bc),
                 reads=[xb], writes=[sqb])
        else:
            P.op("act", lambda e, sq=sq, xt=xt: e.activation(out=sq[:], in_=xt, func=AF.Square),
                 reads=[xb], writes=[sqb])
        for h in range(2):
            P.op("pe", lambda e, h=h, sq=sq, kc=kc: e.matmul(C.ps[6 + h][:], lhsT=C.ones[:], rhs=sq[:, h * 512:(h + 1) * 512],
                                                          start=(kc == 0), stop=(kc == nkc - 1)),
                 reads=[sqb, C.ones_b], writes=[C.psb[6 + h]])
    for h in range(2):
        sl = slice(h * 512, (h + 1) * 512)
        P.op("act", lambda e, h=h, sl=sl: e.activation(out=out_rstd[:, sl], in_=C.ps[6 + h][:], func=AF.Sqrt,
                                                      bias=C.eps_t[:, 0:1], scale=1.0 / nfeat),
             reads=[C.psb[6 + h], C.eps_b], writes=[out_b])
    P.op("dve", lambda e: e.reciprocal(out=out_rstd[:], in_=out_rstd[:]), reads=[out_b], writes=[out_b])


def gemm(C, *, K, w_ap, slabs, at_res=None, at_res_b=None, at_dram=None, at_dram_b=None, mode_of, epilogue, kbs=16):
    P = C.P
    nkc = K // 128
    nkb = nkc // kbs
    hk = kbs // 2
    wv = w_ap.rearrange("(kc p) n -> p kc n", p=128)
    wr = C.wring()
    plan = [(s, kb) for s in slabs for kb in range(nkb)]
    loaded = {}

    def load(i):
        s, kb = plan[i]
        wt, wb = wr.next()
        for q in range(2):
            P.dma("pool", wt[:, q * hk:(q + 1) * hk, :], wv[:, kb * kbs + q * hk: kb * kbs + (q + 1) * hk, s * 512:(s + 1) * 512],
                  writes=[wb], key=wb)
        at = None
        if at_dram is not None:
            att, atb = C.atring().next()
            av = at_dram.rearrange("(kc p) t -> p kc t", p=128)
            for q in range(2):
                P.dma("sp", att[:, q * hk:(q + 1) * hk, :], av[:, kb * kbs + q * hk: kb * kbs + (q + 1) * hk, :],
                      reads=[at_dram_b] if at_dram_b is not None else [], writes=[atb], key=atb)
            at = (att, atb)
        loaded[i] = (wt, wb, at)

    load(0)
    for i, (s, kb) in enumerate(plan):
        if i + 1 < len(plan):
            load(i + 1)
        wt, wb, at = loaded.pop(i)
        mode = mode_of(s)
        for tile in range(8):
            for k16 in range(kbs):
                kc = kb * kbs + k16
                if at is not None:
                    a_t, a_b = at[0], at[1]
                    akc = k16
                else:
                    a_t, a_b = at_res, at_res_b
                    akc = kc
                first = (kc == 0)
                last = (kc == nkc - 1)
                if mode == "fm":
                    nch, half = tile // 2, tile % 2
                    P.op("pe", lambda e, tile=tile, wt=wt, k16=k16, nch=nch, a_t=a_t, akc=akc, half=half, first=first, last=last:
                         e.matmul(C.ps[tile][:], lhsT=wt[:, k16, nch * 128:(nch + 1) * 128],
                                  rhs=a_t[:, akc, half * 512:(half + 1) * 512], start=first, stop=last),
                         reads=[wb, a_b[akc] if isinstance(a_b, list) else a_b], writes=[C.psb[tile]])
                else:
                    P.op("pe", lambda e, tile=tile, wt=wt, k16=k16, a_t=a_t, akc=akc, first=first, last=last:
                         e.matmul(C.ps[tile][:], lhsT=a_t[:, akc, tile * 128:(tile + 1) * 128],
                                  rhs=wt[:, k16, :], start=first, stop=last),
                         reads=[wb, a_b[akc] if isinstance(a_b, list) else a_b], writes=[C.psb[tile]])
            if kb == nkb - 1:
                epilogue(s, tile)


def load_tile(C, dram_ap, dram_b, dt=F32):
    ring = C.f32r if dt == F32 else C.bf1k
    t, b = ring.next()
    C.P.dma("sp", t[:], dram_ap, reads=[dram_b] if dram_b is not None else [], writes=[b], key=b)
    return t, b


def phase_a(C, xT_ap, x_b, rstd, rstd_b, wcol, bcol, tab_b, hT, hT_b, do_stats=True):
    P = C.P
    xv = xT_ap.rearrange("(kc p) t -> kc p t", p=128)
    if do_stats:
        stats_rstd(C, lambda kc: (lambda tb: (tb[0][:], tb[1]))(load_tile(C, xv[kc], x_b)), KC, rstd, rstd_b, D)
    for kc in range(KC):
        xt, xb = load_tile(C, xv[kc], x_b)
        P.op("dve", lambda e, xt=xt, kc=kc: e.scalar_tensor_tensor(out=xt[:], in0=xt[:], scalar=wcol[:, kc:kc + 1], in1=rstd[:],
                                                                  op0=ALU.mult, op1=ALU.mult),
             reads=[xb, rstd_b, tab_b], writes=[xb])
        P.op("act", lambda e, xt=xt, kc=kc: e.activation(out=hT[:, kc, :], in_=xt[:], func=AF.Identity, bias=bcol[:, kc:kc + 1]),
             reads=[xb, tab_b], writes=[hT_b[kc]])


def phase_c(C, yo_ap, yo_b, xT_ap, x_b, xo_ap, xo_b, ggcol, tab_b, rstd, rstd_b, fbcol=None, nstats=True):
    P = C.P
    yv = yo_ap.rearrange("(kc p) t -> kc p t", p=128)
    xv = xT_ap.rearrange("(kc p) t -> kc p t", p=128)
    ov = xo_ap.rearrange("(kc p) t -> kc p t", p=128)
    ry, ryb = C.rstd_y, C.rstd_y_b
    stats_rstd(C, lambda kc: (lambda tb: (tb[0][:], tb[1]))(load_tile(C, yv[kc], yo_b)), KC, ry, ryb, D,
               bias_col=(None if fbcol is None else (lambda kc: fbcol[:, kc:kc + 1])))
    for kc in range(KC):
        yt, yb = load_tile(C, yv[kc], yo_b)
        xt, xb = load_tile(C, xv[kc], x_b)
        if fbcol is None:
            P.op("dve", lambda e, yt=yt, kc=kc: e.scalar_tensor_tensor(out=yt[:], in0=yt[:], scalar=ggcol[:, kc:kc + 1], in1=ry[:],
                                                                      op0=ALU.mult, op1=ALU.mult),
                 reads=[yb, ryb, tab_b], writes=[yb])
        else:
            P.op("dve", lambda e, yt=yt, kc=kc: e.tensor_scalar(out=yt[:], in0=yt[:], scalar1=fbcol[:, kc:kc + 1], scalar2=ggcol[:, kc:kc + 1],
                                                               op0=ALU.add, op1=ALU.mult),
                 reads=[yb, tab_b], writes=[yb])
            P.op("dve", lambda e, yt=yt: e.tensor_tensor(out=yt[:], in0=yt[:], in1=ry[:], op=ALU.mult),
                 reads=[yb, ryb], writes=[yb])
        P.op("dve", lambda e, yt=yt, xt=xt: e.tensor_tensor(out=xt[:], in0=xt[:], in1=yt[:], op=ALU.add),
             reads=[yb, xb], writes=[xb])
        P.dma("sp", ov[kc], xt[:], reads=[xb], writes=[xo_b], key=xb)
        if nstats:
            sq, sqb = C.bf1k.next()
            P.op("act", lambda e, sq=sq, xt=xt: e.activation(out=sq[:], in_=xt[:], func=AF.Square), reads=[xb], writes=[sqb])
            for h in range(2):
                P.op("pe", lambda e, h=h, sq=sq, kc=kc: e.matmul(C.ps[6 + h][:], lhsT=C.ones[:], rhs=sq[:, h * 512:(h + 1) * 512],
                                                              start=(kc == 0), stop=(kc == KC - 1)),
                     reads=[sqb, C.ones_b], writes=[C.psb[6 + h]])
    if nstats:
        for h in range(2):
            sl = slice(h * 512, (h + 1) * 512)
            P.op("act", lambda e, h=h, sl=sl: e.activation(out=rstd[:, sl], in_=C.ps[6 + h][:], func=AF.Sqrt,
                                                          bias=C.eps_t[:, 0:1], scale=1.0 / D),
                 reads=[C.psb[6 + h], C.eps_b], writes=[rstd_b])
        P.op("dve", lambda e: e.reciprocal(out=rstd[:], in_=rstd[:]), reads=[rstd_b], writes=[rstd_b])


def ep_store_f32(C, out_ap, out_b):
    P = C.P
    cnt = [0]

    def ep(s, tile):
        nch, half = tile // 2, tile % 2
        st, sb = C.st32.next()
        eng = "act" if cnt[0] % 2 == 0 else "dve"
        cnt[0] += 1
        if eng == "act":
            P.op("act", lambda e, st=st, tile=tile: e.activation(out=st[:], in_=C.ps[tile][:], func=AF.Copy),
                 reads=[C.psb[tile]], writes=[sb])
        else:
            P.op("dve", lambda e, st=st, tile=tile: e.tensor_copy(out=st[:], in_=C.ps[tile][:]),
                 reads=[C.psb[tile]], writes=[sb])
        r0 = s * 512 + nch * 128
        P.dma("sp", out_ap[r0:r0 + 128, half * 512:(half + 1) * 512], st[:], reads=[sb], writes=[out_b], key=sb)
    return ep


def ep_act_bf16(C, out_ap, out_b, func, row_off=0, square_after=False):
    P = C.P

    def ep(s, tile):
        nch, half = tile // 2, tile % 2
        st, sb = C.st16.next()
        if square_after:
            tt, tb = C.st32.next()
            P.op("act", lambda e, tt=tt, tile=tile: e.activation(out=tt[:], in_=C.ps[tile][:], func=func),
                 reads=[C.psb[tile]], writes=[tb])
            P.op("dve", lambda e, st=st, tt=tt: e.tensor_tensor(out=st[:], in0=tt[:], in1=tt[:], op=ALU.mult),
                 reads=[tb], writes=[sb])
        else:
            P.op("act", lambda e, st=st, tile=tile: e.activation(out=st[:], in_=C.ps[tile][:], func=func),
                 reads=[C.psb[tile]], writes=[sb])
        r0 = s * 512 + nch * 128 - row_off
        P.dma("sp", out_ap[r0:r0 + 128, half * 512:(half + 1) * 512], st[:], reads=[sb], writes=[out_b], key=sb)
    return ep


def load_tab(C, name, dram_ap, shape):
    t = C.P.sbuf(name + "_sb", shape, F32)
    b = Buf(name)
    C.P.dma("sp", t[:], dram_ap, writes=[b], key=b)
    return t, b


def setup_eps(C):
    C.eps_t = C.P.sbuf("eps", [128, 1], F32)
    C.eps_b = Buf("eps")
    C.P.op("pool", lambda e: e.memset(C.eps_t[:], EPS), writes=[C.eps_b])
    C.rstd_y = C.P.sbuf("rstd_y", [128, T], F32)
    C.rstd_y_b = Buf("rstd_y")


NJ = 6 * D // 128 // NCORE


def build_l1():
    nc = bass.Bass("TRN2", target_bir_lowering=False)
    cT = nc.dram_tensor("cT", [128, KC], F32, kind="ExternalInput").ap()
    w = nc.dram_tensor("w", [2, D, NJ * 128], F32, kind="ExternalInput").ap()
    b = nc.dram_tensor("b", [128, 2 * NJ], F32, kind="ExternalInput").ap()
    out = nc.dram_tensor("mod", [128, 2 * NJ], F32, kind="ExternalOutput").ap()
    P = Prog(nc)
    ct = P.sbuf("ct", [128, KC], F32)
    ctb = Buf("ct")
    sg = P.sbuf("sg", [128, KC], F32)
    bt = P.sbuf("bt", [128, 2 * NJ], F32)
    btb = Buf("bt")
    res = P.sbuf("res", [128, 2 * NJ], F32)
    resb = Buf("res")
    pm = P.psum("pm", [128, 2 * NJ], F32)
    pmb = Buf("pm")
    wr = Ring(P, "w", 4, [128, KC, 128], F32)
    P.dma("sp", ct[:], cT, writes=[ctb], key=ctb)
    P.dma("sp", bt[:], b, writes=[btb], key=btb)
    P.op("act", lambda e: e.activation(out=sg[:], in_=ct[:], func=AF.Silu), reads=[ctb], writes=[ctb])
    for l in range(2):
        wv = w[l].rearrange("(kc p) n -> p kc n", p=128)
        for j in range(NJ):
            wt, wb = wr.next()
            q = "sp" if j % 2 == 0 else "act"
            P.dma(q, wt[:], wv[:, :, j * 128:(j + 1) * 128], writes=[wb], key=wb)
            col = l * NJ + j
            for kc in range(KC):
                P.op("pe", lambda e, wt=wt, kc=kc, col=col: e.matmul(pm[:, col:col + 1], lhsT=wt[:, kc, :], rhs=sg[:, kc:kc + 1],
                                                                  start=(kc == 0), stop=(kc == KC - 1)),
                     reads=[wb, ctb], writes=[pmb])
    P.op("dve", lambda e: e.tensor_tensor(out=res[:], in0=pm[:], in1=bt[:], op=ALU.add), reads=[pmb, btb], writes=[resb])
    P.dma("sp", out, res[:], reads=[resb], key=resb)
    P.emit(final_waits=[resb])
    return nc


def run_l1(c, ada_w, ada_b):
    cT = np.ascontiguousarray(c.reshape(KC, 128).T)
    in_maps = []
    for i in range(NCORE):
        cols = slice(i * NJ * 128, (i + 1) * NJ * 128)
        w = np.ascontiguousarray(ada_w[:, :, cols])
        b = ada_b[:, cols].reshape(2, NJ, 128).transpose(2, 0, 1).reshape(128, 2 * NJ)
        in_maps.append({"cT": cT, "w": w, "b": np.ascontiguousarray(b)})
    nc = build_l1()
    res = run_bass_kernel_spmd(nc, in_maps, core_ids=list(range(NCORE)))
    mod = np.zeros((2, 6 * D), np.float32)
    for i in range(NCORE):
        m = res.results[i]["mod"].reshape(128, 2, NJ)
        mod[:, i * NJ * 128:(i + 1) * NJ * 128] = m.transpose(1, 2, 0).reshape(2, NJ * 128)
    return mod.reshape(2, 6, D)


def col_layout(v):
    v = np.asarray(v, np.float32)
    lead = v.shape[:-1]
    n = v.shape[-1] // 128
    r = v.reshape(lead + (n, 128))
    r = np.moveaxis(r, -1, 0)
    return np.ascontiguousarray(r)


def rope_tables():
    half = DK // 2
    inv = (np.float32(ROPE_BASE) ** (-np.arange(half, dtype=np.float32) / np.float32(half))).astype(np.float32)
    pos = np.arange(SEQ, dtype=np.float32)
    ang = (pos[:, None] * inv[None, :]).astype(np.float32)
    cos = np.cos(ang.astype(np.float64)).astype(np.float32).T
    sin = np.sin(ang.astype(np.float64)).astype(np.float32).T
    return np.ascontiguousarray(cos), np.ascontiguousarray(sin)


def build_l2():
    nc = bass.Bass("TRN2", target_bir_lowering=False)
    xT = nc.dram_tensor("xT", [D, T], F32, kind="ExternalInput").ap()
    tabs = nc.dram_tensor("tabs", [128, 3, KC], F32, kind="ExternalInput").ap()
    rope = nc.dram_tensor("rope", [128, 4, T], F32, kind="ExternalInput").ap()
    w_in = nc.dram_tensor("w_in", [D, 6 * D], F32, kind="ExternalInput").ap()
    qT = nc.dram_tensor("qT", [D, T], BF16, kind="ExternalOutput").ap()
    kT = nc.dram_tensor("kT", [D, T], BF16, kind="ExternalOutput").ap()
    v = nc.dram_tensor("v", [T, VW], BF16, kind="ExternalOutput").ap()
    sgT = nc.dram_tensor("sgT", [VW, T], BF16, kind="ExternalOutput").ap()
    P = Prog(nc)
    C = Ctx(P)
    setup_eps(C)
    tb, tbb = load_tab(C, "tabs", tabs, [128, 3, KC])
    rp, rpb = load_tab(C, "rope", rope, [128, 4, T])
    wcol = P.sbuf("wcol", [128, KC], F32)
    P.op("dve", lambda e: e.scalar_tensor_tensor(out=wcol[:], in0=tb[:, 0, :], scalar=1.0, in1=tb[:, 2, :], op0=ALU.add, op1=ALU.mult),
         reads=[tbb], writes=[tbb])
    hT = P.sbuf("hT", [128, KC, T], BF16)
    hTb = P.bufs("hT", KC)
    rstd = P.sbuf("rstd", [128, T], F32)
    rstdb = Buf("rstd")
    phase_a(C, xT, None, rstd, rstdb, wcol, tb[:, 1, :], tbb, hT, hTb)
    outb = Buf("outs")

    def mode_of(s):
        return "tm" if 16 <= s < 32 else "fm"

    ep_gate = ep_act_bf16(C, sgT, outb, AF.Silu, row_off=2 * D + VW)

    def ep(s, tile):
        if s < 16:
            nch, half = tile // 2, tile % 2
            if nch % 2 == 0:
                return
            t1, t2 = tile - 2, tile
            sl = slice(half * 512, (half + 1) * 512)
            ci, si = (0, 1) if s < 8 else (2, 3)
            dst = qT if s < 8 else kT
            r0 = (s % 8) * 512 + (nch - 1) * 128
            def rot(ta, tb_, op, dst_ap):
                a, ab = C.st32.next()
                b2, bb = C.st32.next()
                o, ob = C.st16.next()
                P.op("dve", lambda e: e.tensor_tensor(out=a[:], in0=C.ps[t1][:], in1=rp[:, ta, sl], op=ALU.mult), reads=[C.psb[t1], rpb], writes=[ab])
                P.op("dve", lambda e: e.tensor_tensor(out=b2[:], in0=C.ps[t2][:], in1=rp[:, tb_, sl], op=ALU.mult), reads=[C.psb[t2], rpb], writes=[bb])
                P.op("dve", lambda e: e.tensor_tensor(out=o[:], in0=a[:], in1=b2[:], op=op), reads=[ab, bb], writes=[ob])
                P.dma("sp", dst_ap, o[:], reads=[ob], writes=[outb], key=ob)
            rot(ci, si, ALU.subtract, dst[r0:r0 + 128, sl])
            rot(si, ci, ALU.add, dst[r0 + 128:r0 + 256, sl])
        elif s < 32:
            st, sb = C.st16.next()
            P.op("act", lambda e: e.activation(out=st[:], in_=C.ps[tile][:], func=AF.Copy), reads=[C.psb[tile]], writes=[sb])
            c0 = (s - 16) * 512
            P.dma("sp", v[tile * 128:(tile + 1) * 128, c0:c0 + 512], st[:], reads=[sb], writes=[outb], key=sb)
        else:
            ep_gate(s, tile)

    gemm(C, K=D, w_ap=w_in, slabs=list(range(48)), at_res=hT, at_res_b=hTb, mode_of=mode_of, epilogue=ep)
    P.emit(final_waits=C.st16.b + C.st32.b)
    return nc


def run_l2(x, mod, norm_g, w_in):
    cos, sin = rope_tables()
    tabs = np.stack([col_layout(mod[0, 1]), col_layout(mod[0, 0]), col_layout(norm_g[0, 0])], axis=1)
    in_maps = []
    for i in range(NCORE):
        tok = slice(i * T, (i + 1) * T)
        rope = np.stack([cos[:, tok], sin[:, tok], cos[:, tok] / 16.0, sin[:, tok] / 16.0], axis=1).astype(np.float32)
        in_maps.append({"xT": np.ascontiguousarray(x[tok].T), "tabs": tabs, "rope": np.ascontiguousarray(rope), "w_in": w_in})
    nc = build_l2()
    res = run_bass_kernel_spmd(nc, in_maps, core_ids=list(range(NCORE)))
    return res.results


NCH = SEQ // 128
SUP = 4


def ret_consts():
    j = np.arange(128, dtype=np.float32)[:, None]
    i = np.arange(128, dtype=np.float32)[None, :]
    cst = np.zeros((128, 6, 128), np.float32)
    cst[:, 0] = np.maximum(i - j, 0)
    cst[:, 1] = (i >= j)
    cst[:, 2] = np.maximum(j - i, 0)
    cst[:, 3] = (j >= i)
    cst[:, 4] = np.broadcast_to(i + 1.0, (128, 128))
    cst[:, 5] = np.broadcast_to(128.0 - i, (128, 128))
    col = np.zeros((128, 2), np.float32)
    col[:, 0] = 127.0 - np.arange(128)
    col[:, 1] = np.arange(128)
    ident = np.eye(128, dtype=np.float32).astype(NPBF)
    return cst, col, ident


def build_l3():
    nc = bass.Bass("TRN2", target_bir_lowering=False)
    qT = nc.dram_tensor("qT", [2, DK, SEQ], BF16, kind="ExternalInput").ap()
    kT = nc.dram_tensor("kT", [2, DK, SEQ], BF16, kind="ExternalInput").ap()
    v = nc.dram_tensor("v", [SEQ, 2 * DV], BF16, kind="ExternalInput").ap()
    dec = nc.dram_tensor("dec", [128, 4], F32, kind="ExternalInput").ap()
    cst_d = nc.dram_tensor("cst", [128, 6, 128], F32, kind="ExternalInput").ap()
    col_d = nc.dram_tensor("col", [128, 2], F32, kind="ExternalInput").ap()
    ident_d = nc.dram_tensor("ident", [128, 128], BF16, kind="ExternalInput").ap()
    yT = nc.dram_tensor("yT", [2, DV, SEQ], BF16, kind="ExternalOutput").ap()
    sbd = nc.dram_tensor("sbd", [2, NCH, 128, 2 * DV], BF16).ap()
    P = Prog(nc)
    fw = emit_retention(P, qT, kT, v, dec, cst_d, col_d, ident_d, yT, sbd)
    P.emit(final_waits=fw)
    return nc


def run_l3(qTh, kTh, vh, dec_f, dec_b):
    cst, col, ident = ret_consts()
    in_maps = []
    for i in range(NCORE):
        dec = np.array([dec_f[2 * i], dec_f[2 * i + 1], dec_b[2 * i], dec_b[2 * i + 1]], np.float32)
        in_maps.append({"qT": np.ascontiguousarray(qTh[2 * i:2 * i + 2]), "kT": np.ascontiguousarray(kTh[2 * i:2 * i + 2]),
                        "v": np.ascontiguousarray(vh[:, i * 2 * DV:(i + 1) * 2 * DV]),
                        "dec": np.ascontiguousarray(np.broadcast_to(dec, (128, 4))), "cst": cst, "col": col, "ident": ident})
    nc = build_l3()
    res = run_bass_kernel_spmd(nc, in_maps, core_ids=list(range(NCORE)))
    return np.concatenate([res.results[i]["yT"].reshape(2 * DV, SEQ) for i in range(NCORE)], axis=0)


import os
RET_STAGE = int(os.environ.get("RET_STAGE", "9"))


def emit_retention(P, qT, kT, v, dec, cst_d, col_d, ident_d, yT, sbd, in_bufs=(), out_buf=None):
    banks = [P.psum("rb%d" % i, [128, 512], F32) for i in range(8)]
    in_bufs = list(in_bufs)
    eps_t = P.sbuf("r_eps", [128, 1], F32)
    epsb = Buf("r_eps")
    P.op("pool", lambda e: e.memset(eps_t[:], EPS), writes=[epsb])

    def tab(name, shape, dt, src):
        t = P.sbuf(name, shape, dt)
        b = Buf(name)
        P.dma("sp", t[:], src, writes=[b], key=b)
        return t, b
    cst, cstb = tab("r_cst", [128, 6, 128], F32, cst_d)
    col, colb = tab("r_col", [128, 2], F32, col_d)
    ident, identb = tab("r_ident", [128, 128], BF16, ident_d)
    dx, dxb = tab("r_dec", [128, 4], F32, dec)
    sm = {n: P.sbuf("r_" + n, [128, 4], F32) for n in ("ax", "u", "z", "z2", "p", "mn", "lg")}
    smb = Buf("r_small")

    def dv(fn):
        P.op("dve", fn, reads=[smb, dxb], writes=[smb])
    P.op("act", lambda e: e.activation(out=sm["ax"][:], in_=dx[:], func=AF.Abs), reads=[smb, dxb], writes=[smb])
    P.op("act", lambda e: e.activation(out=sm["u"][:], in_=sm["ax"][:], func=AF.Exp, scale=-1.0), reads=[smb], writes=[smb])
    dv(lambda e: e.tensor_scalar(out=sm["z"][:], in0=sm["u"][:], scalar1=2.0, scalar2=None, op0=ALU.add))
    dv(lambda e: e.reciprocal(out=sm["z"][:], in_=sm["z"][:]))
    dv(lambda e: e.tensor_tensor(out=sm["z"][:], in0=sm["z"][:], in1=sm["u"][:], op=ALU.mult))
    dv(lambda e: e.tensor_tensor(out=sm["z2"][:], in0=sm["z"][:], in1=sm["z"][:], op=ALU.mult))
    dv(lambda e: e.tensor_scalar(out=sm["p"][:], in0=sm["z2"][:], scalar1=1.0 / 11.0, scalar2=1.0 / 9.0, op0=ALU.mult, op1=ALU.add))
    for cc in (1.0 / 7.0, 1.0 / 5.0, 1.0 / 3.0, 1.0):
        dv(lambda e: e.tensor_tensor(out=sm["p"][:], in0=sm["p"][:], in1=sm["z2"][:], op=ALU.mult))
        dv(lambda e, cc=cc: e.tensor_scalar(out=sm["p"][:], in0=sm["p"][:], scalar1=cc, scalar2=None, op0=ALU.add))
    dv(lambda e: e.tensor_tensor(out=sm["p"][:], in0=sm["p"][:], in1=sm["z"][:], op=ALU.mult))
    dv(lambda e: e.tensor_scalar(out=sm["mn"][:], in0=dx[:], scalar1=0.0, scalar2=None, op0=ALU.min))
    dv(lambda e: e.scalar_tensor_tensor(out=sm["lg"][:], in0=sm["p"][:], scalar=-2.0, in1=sm["mn"][:], op0=ALU.mult, op1=ALU.add))
    lg = sm["lg"]
    Mt, qdec, kdec, cdec = {}, {}, {}, {}
    tabb = Buf("r_tabs")
    e1 = P.sbuf("r_e1", [128, 128], F32)
    e2 = P.sbuf("r_e2", [128, 128], F32)
    eb = Buf("r_e")
    for hh in range(2):
        Mt[hh] = P.sbuf("r_Mt%d" % hh, [128, 128], F32)
        lf = lg[:, hh:hh + 1]
        lb = lg[:, 2 + hh:3 + hh]
        P.op("act", lambda e, lf=lf: e.activation(out=e1[:], in_=cst[:, 0, :], func=AF.Exp, scale=lf), reads=[smb, cstb], writes=[eb])
        P.op("act", lambda e, lb=lb: e.activation(out=e2[:], in_=cst[:, 2, :], func=AF.Exp, scale=lb), reads=[smb, cstb], writes=[eb])
        P.op("dve", lambda e: e.tensor_tensor(out=e1[:], in0=e1[:], in1=cst[:, 1, :], op=ALU.mult), reads=[eb, cstb], writes=[eb])
        P.op("dve", lambda e: e.tensor_tensor(out=e2[:], in0=e2[:], in1=cst[:, 3, :], op=ALU.mult), reads=[eb, cstb], writes=[eb])
        P.op("dve", lambda e, hh=hh: e.tensor_tensor(out=Mt[hh][:], in0=e1[:], in1=e2[:], op=ALU.add), reads=[eb], writes=[tabb, eb])
        for d, l_ in ((0, lf), (1, lb)):
            qdec[hh, d] = P.sbuf("r_qd%d%d" % (hh, d), [128, 128], F32)
            kdec[hh, d] = P.sbuf("r_kd%d%d" % (hh, d), [128, 1], F32)
            cdec[hh, d] = P.sbuf("r_cd%d%d" % (hh, d), [128, 1], F32)
            P.op("act", lambda e, hh=hh, d=d, l_=l_: e.activation(out=qdec[hh, d][:], in_=cst[:, 4 + d, :], func=AF.Exp, scale=l_),
                 reads=[smb, cstb], writes=[tabb])
            P.op("act", lambda e, hh=hh, d=d, l_=l_: e.activation(out=kdec[hh, d][:], in_=col[:, d:d + 1], func=AF.Exp, scale=l_),
                 reads=[smb, colb], writes=[tabb])
            P.op("act", lambda e, hh=hh, d=d, l_=l_: e.activation(out=cdec[hh, d][:], in_=l_, func=AF.Exp, scale=128.0),
                 reads=[smb], writes=[tabb])
    S = [P.sbuf("r_S%d" % hh, [128, 2, DV], F32) for hh in range(2)]
    Sb = P.bufs("r_S", 2)
    Sbf = [Ring(P, "r_Sbf%d" % hh, 2, [128, 2, DV], BF16) for hh in range(2)]
    kring = [Ring(P, "r_k%d" % hh, 2, [128, 2, SUP * 128], BF16) for hh in range(2)]
    qring = [Ring(P, "r_q%d" % hh, 2, [128, 2, SUP * 128], BF16) for hh in range(2)]
    vring = [Ring(P, "r_v%d" % hh, 2, [128, SUP, DV], BF16) for hh in range(2)]
    sdring = [Ring(P, "r_sd%d" % hh, 2, [128, 2, DV], BF16) for hh in range(2)]
    kbring = [Ring(P, "r_kb%d" % hh, 2, [128, 256], BF16) for hh in range(2)]
    smring = [Ring(P, "r_sm%d" % hh, 2, [128, 128], BF16) for hh in range(2)]
    qfring = [Ring(P, "r_qf%d" % hh, 2, [128, 2, 2, 128], BF16) for hh in range(2)]
    ynring = [Ring(P, "r_yn%d" % hh, 2, [128, DV], BF16) for hh in range(2)]
    ytring = [Ring(P, "r_yt%d" % hh, 2, [128, 4, 128], BF16) for hh in range(2)]
    stt = [P.sbuf("r_st%d" % hh, [128, 8], F32) for hh in range(2)]
    sttb = P.bufs("r_st", 2)
    psu = [[banks[4 * hh + 0], banks[4 * hh + 1]] for hh in range(2)]
    psy = [banks[4 * hh + 2] for hh in range(2)]
    pst = [banks[4 * hh + 3][:, 0:128].bitcast(BF16) for hh in range(2)]
    pss = [banks[4 * hh + 3][:, 128:256] for hh in range(2)]
    pyt = [banks[4 * hh + 3][:, 256:512].bitcast(BF16) for hh in range(2)]
    psub = [P.bufs("psu%d" % hh, 2) for hh in range(2)]
    psyb = P.bufs("psy", 2)
    pstb = P.bufs("bankD", 2)
    pssb = pstb
    pytb = pstb
    sbd_b = [[Buf("sbd%d_%d" % (hh, c)) for c in range(NCH)] for hh in range(2)]

    kv = [kT[hh].rearrange("(dc p) t -> p dc t", p=128) for hh in range(2)]
    qv = [qT[hh].rearrange("(dc p) t -> p dc t", p=128) for hh in range(2)]
    vv = v.rearrange("(c p) e -> p c e", p=128)
    yv = [yT[hh].rearrange("(ec p) t -> p ec t", p=128) for hh in range(2)]

    def zero_state(hh):
        P.op("pool", lambda e: e.memset(S[hh][:], 0.0), writes=[Sb[hh]])
        t, b = Sbf[hh].next()
        P.op("pool", lambda e: e.memset(t[:], 0.0), writes=[b])
        return t, b

    def load_kv(hh, s4, need_q):
        kt, kb_ = kring[hh].next()
        P.dma("sp", kt[:], kv[hh][:, :, s4 * SUP * 128:(s4 + 1) * SUP * 128], reads=in_bufs, writes=[kb_], key=kb_)
        vt, vb = vring[hh].next()
        P.dma("sp", vt[:], vv[:, s4 * SUP:(s4 + 1) * SUP, hh * DV:(hh + 1) * DV], reads=in_bufs, writes=[vb], key=vb)
        q = None
        if need_q:
            qt, qb = qring[hh].next()
            P.dma("sp", qt[:], qv[hh][:, :, s4 * SUP * 128:(s4 + 1) * SUP * 128], reads=in_bufs, writes=[qb], key=qb)
            q = (qt, qb)
        return (kt, kb_), (vt, vb), q

    def update(hh, d, kt, kb_, vt, vb, ci, cur):
        for dc in range(2):
            P.op("pe", lambda e, dc=dc: e.transpose(out=pst[hh][:, dc * 128:(dc + 1) * 128], in_=kt[:, dc, ci * 128:(ci + 1) * 128], identity=ident[:]),
                 reads=[kb_, identb], writes=[pstb[hh]])
        kbt, kbb = kbring[hh].next()
        P.op("dve", lambda e: e.tensor_scalar(out=kbt[:], in0=pst[hh], scalar1=kdec[hh, d][:, 0:1], scalar2=None, op0=ALU.mult),
             reads=[pstb[hh], tabb], writes=[kbb])
        for dc in range(2):
            P.op("pe", lambda e, dc=dc: e.matmul(psu[hh][dc][:], lhsT=kbt[:, dc * 128:(dc + 1) * 128], rhs=vt[:, ci, :], start=True, stop=True),
                 reads=[kbb, vb], writes=[psub[hh][dc]])
        for dc in range(2):
            P.op("dve", lambda e, dc=dc: e.scalar_tensor_tensor(out=S[hh][:, dc, :], in0=S[hh][:, dc, :], scalar=cdec[hh, d][:, 0:1],
                                                               in1=psu[hh][dc][:], op0=ALU.mult, op1=ALU.add),
                 reads=[Sb[hh], psub[hh][dc], tabb], writes=[Sb[hh]])
        nt, nb = Sbf[hh].next()
        P.op("act", lambda e: e.activation(out=nt[:], in_=S[hh][:], func=AF.Copy), reads=[Sb[hh]], writes=[nb])
        return nt, nb

    if RET_STAGE < 1:
        return []
    cur = [zero_state(hh) for hh in range(2)]
    for s4 in range(NCH // SUP - 1, -1, -1):
        ld = [load_kv(hh, s4, False) for hh in range(2)]
        for ci in range(SUP - 1, -1, -1):
            c = s4 * SUP + ci
            for hh in range(2):
                (kt, kb_), (vt, vb), _ = ld[hh]
                if c < NCH - 1 and RET_STAGE >= 2:
                    P.dma("sp", sbd[hh, c], cur[hh][0][:].rearrange("p a b -> p (a b)"), reads=[cur[hh][1]], writes=[sbd_b[hh][c]], key=cur[hh][1])
                if c > 0:
                    cur[hh] = update(hh, 1, kt, kb_, vt, vb, ci, cur[hh])
    if RET_STAGE < 3:
        return []
    cur = [zero_state(hh) for hh in range(2)]
    for s4 in range(NCH // SUP):
        ld = [load_kv(hh, s4, True) for hh in range(2)]
        for ci in range(SUP):
            c = s4 * SUP + ci
            for hh in range(2):
                (kt, kb_), (vt, vb), (qt, qb) = ld[hh]
                tsl = slice(ci * 128, (ci + 1) * 128)
                sd = None
                if c < NCH - 1:
                    sdt, sdb = sdring[hh].next()
                    P.dma("sp", sdt[:].rearrange("p a b -> p (a b)"), sbd[hh, c], reads=[sbd_b[hh][c]], writes=[sdb], key=sdb)
                    sd = (sdt, sdb)
                for dc in range(2):
                    P.op("pe", lambda e, dc=dc, kt=kt, qt=qt, tsl=tsl, hh=hh: e.matmul(pss[hh], lhsT=kt[:, dc, tsl], rhs=qt[:, dc, tsl], start=(dc == 0), stop=(dc == 1)),
                         reads=[kb_, qb], writes=[pssb[hh]])
                smt, smb_ = smring[hh].next()
                P.op("dve", lambda e, smt=smt, hh=hh: e.tensor_tensor(out=smt[:], in0=pss[hh], in1=Mt[hh][:], op=ALU.mult),
                     reads=[pssb[hh], tabb], writes=[smb_])
                qft, qfb = qfring[hh].next()
                for d in range(2):
                    for dc in range(2):
                        P.op("dve", lambda e, d=d, dc=dc, qft=qft, qt=qt, tsl=tsl, hh=hh: e.tensor_tensor(out=qft[:, d, dc, :], in0=qt[:, dc, tsl], in1=qdec[hh, d][:], op=ALU.mult),
                             reads=[qb, tabb], writes=[qfb])
                mms = [(smt, vt[:, ci, :], [smb_, vb])]
                if c > 0:
                    for dc in range(2):
                        mms.append((qft[:, 0, dc, :], cur[hh][0][:, dc, :], [qfb, cur[hh][1]]))
                if sd is not None:
                    for dc in range(2):
                        mms.append((qft[:, 1, dc, :], sd[0][:, dc, :], [qfb, sd[1]]))
                for i, (l_, r_, rd) in enumerate(mms):
                    l_ap = l_[:] if l_ is smt else l_
                    P.op("pe", lambda e, l_ap=l_ap, r_=r_, i=i, n=len(mms), hh=hh: e.matmul(psy[hh][:], lhsT=l_ap, rhs=r_, start=(i == 0), stop=(i == n - 1)),
                         reads=rd, writes=[psyb[hh]])
                if RET_STAGE < 4:
                    continue
                P.op("dve", lambda e, hh=hh: e.bn_stats(out=stt[hh][:, 0:6], in_=psy[hh][:]), reads=[psyb[hh]], writes=[sttb[hh]])
                P.op("dve", lambda e, hh=hh: e.bn_aggr(out=stt[hh][:, 6:8], in_=stt[hh][:, 0:6]), reads=[sttb[hh]], writes=[sttb[hh]])
                P.op("act", lambda e, hh=hh: e.activation(out=stt[hh][:, 7:8], in_=stt[hh][:, 7:8], func=AF.Sqrt, bias=eps_t[:, 0:1]),
                     reads=[sttb[hh], epsb], writes=[sttb[hh]])
                P.op("dve", lambda e, hh=hh: e.reciprocal(out=stt[hh][:, 7:8], in_=stt[hh][:, 7:8]), reads=[sttb[hh]], writes=[sttb[hh]])
                ynt, ynb = ynring[hh].next()
                P.op("dve", lambda e, hh=hh, ynt=ynt: e.tensor_scalar(out=ynt[:], in0=psy[hh][:], scalar1=stt[hh][:, 6:7], scalar2=stt[hh][:, 7:8],
                                                                  op0=ALU.subtract, op1=ALU.mult),
                     reads=[psyb[hh], sttb[hh]], writes=[ynb])
                if RET_STAGE < 5:
                    continue
                for ec in range(4):
                    P.op("pe", lambda e, ec=ec, hh=hh, ynt=ynt: e.transpose(out=pyt[hh][:, ec * 128:(ec + 1) * 128], in_=ynt[:, ec * 128:(ec + 1) * 128], identity=ident[:]),
                         reads=[ynb, identb], writes=[pytb[hh]])
                ytt, ytb = ytring[hh].next()
                P.op("act", lambda e, hh=hh, ytt=ytt: e.activation(out=ytt[:].rearrange("p a b -> p (a b)"), in_=pyt[hh], func=AF.Copy),
                     reads=[pytb[hh]], writes=[ytb])
                P.dma("sp", yv[hh][:, :, c * 128:(c + 1) * 128], ytt[:], reads=[ytb], writes=[out_buf] if out_buf is not None else [], key=ytb)
                if c < NCH - 1 and RET_STAGE >= 6:
                    cur[hh] = update(hh, 0, kt, kb_, vt, vb, ci, cur[hh])
    return [b for hh in range(2) for b in ytring[hh].b]


def tab_prod(C, tb, tbb, name, ia, ib, plus1):
    o = C.P.sbuf(name, [128, KC], F32)
    if plus1:
        C.P.op("dve", lambda e: e.scalar_tensor_tensor(out=o[:], in0=tb[:, ia, :], scalar=1.0, in1=tb[:, ib, :], op0=ALU.add, op1=ALU.mult),
               reads=[tbb], writes=[tbb])
    else:
        C.P.op("dve", lambda e: e.tensor_tensor(out=o[:], in0=tb[:, ia, :], in1=tb[:, ib, :], op=ALU.mult), reads=[tbb], writes=[tbb])
    return o


def fm(_s):
    return "fm"


def mlp_block(C, x_ap, x_b, xo_ap, xo_b, rstd, rstd_b, wcol, bcol, ggcol, tbb, hT, hTb, w1, w2, aT, aT_b, yo, yo_b, nstats):
    phase_a(C, x_ap, x_b, rstd, rstd_b, wcol, bcol, tbb, hT, hTb, do_stats=False)
    gemm(C, K=D, w_ap=w1, slabs=list(range(DFF // 512)), at_res=hT, at_res_b=hTb, mode_of=fm,
         epilogue=ep_act_bf16(C, aT, aT_b, AF.Relu, square_after=True))
    gemm(C, K=DFF, w_ap=w2, slabs=list(range(D // 512)), at_dram=aT, at_dram_b=aT_b, mode_of=fm,
         epilogue=ep_store_f32(C, yo, yo_b), kbs=8)
    phase_c(C, yo, yo_b, x_ap, x_b, xo_ap, xo_b, ggcol, tbb, rstd, rstd_b, nstats=nstats)


def dft_consts():
    f = np.arange(512)
    ang = 2 * np.pi * np.outer(f, f) / 512.0
    cs = np.concatenate([np.cos(ang), -np.sin(ang)], axis=1).astype(np.float32)
    cs = cs.reshape(4, 128, 1024).transpose(1, 0, 2)
    return np.ascontiguousarray(cs).astype(NPBF)


def build_l4():
    nc = bass.Bass("TRN2", target_bir_lowering=False)
    xT = nc.dram_tensor("xT", [D, T], F32, kind="ExternalInput").ap()
    yT = nc.dram_tensor("yT", [VW, T], BF16, kind="ExternalInput").ap()
    sgT = nc.dram_tensor("sgT", [VW, T], BF16, kind="ExternalInput").ap()
    gng = nc.dram_tensor("gng", [128, VW // 128], F32, kind="ExternalInput").ap()
    tabs = nc.dram_tensor("tabs", [128, 10, KC], F32, kind="ExternalInput").ap()
    w_out = nc.dram_tensor("w_out", [VW, D], F32, kind="ExternalInput").ap()
    w1 = nc.dram_tensor("w1", [D, DFF], F32, kind="ExternalInput").ap()
    w2 = nc.dram_tensor("w2", [DFF, D], F32, kind="ExternalInput").ap()
    cs_d = nc.dram_tensor("cs", [128, 4, 1024], BF16, kind="ExternalInput").ap()
    x1T = nc.dram_tensor("x1T", [D, T], F32, kind="ExternalOutput").ap()
    Z = nc.dram_tensor("Z", [8, 2, T, 512], BF16, kind="ExternalOutput").ap()
    ygT = nc.dram_tensor("ygT", [VW, T], BF16).ap()
    yo = nc.dram_tensor("yo", [D, T], F32).ap()
    xs = nc.dram_tensor("xs", [D, T], F32).ap()
    aT = nc.dram_tensor("aT", [DFF, T], BF16).ap()
    P = Prog(nc)
    C = Ctx(P)
    setup_eps(C)
    tb, tbb = load_tab(C, "tabs", tabs, [128, 10, KC])
    gn, gnb = load_tab(C, "gng", gng, [128, VW // 128])
    cs = P.sbuf("cs_sb", [128, 4, 1024], BF16)
    csb = Buf("cs")
    P.dma("sp", cs[:], cs_d, writes=[csb], key=csb)
    gg1 = tab_prod(C, tb, tbb, "gg1", 0, 1, False)
    wc2 = tab_prod(C, tb, tbb, "wc2", 2, 4, True)
    gg2 = tab_prod(C, tb, tbb, "gg2", 5, 6, False)
    wc1n = tab_prod(C, tb, tbb, "wc1n", 7, 9, True)
    hT = P.sbuf("hT", [128, KC, T], BF16)
    hTb = P.bufs("hT", KC)
    rstd = P.sbuf("rstd", [128, T], F32)
    rstdb = Buf("rstd")
    yg_b, yo_b, xs_b, aT_b, x1_b, z_b = Buf("ygT"), Buf("yo"), Buf("xs"), Buf("aT"), Buf("x1T"), Buf("Z")
    yv = yT.rearrange("(kc p) t -> kc p t", p=128)
    sv = sgT.rearrange("(kc p) t -> kc p t", p=128)
    gv = ygT.rearrange("(kc p) t -> kc p t", p=128)
    for kc in range(VW // 128):
        yt, yb = load_tile(C, yv[kc], None, BF16)
        st, sb = load_tile(C, sv[kc], None, BF16)
        P.op("dve", lambda e, yt=yt, st=st, kc=kc: e.scalar_tensor_tensor(out=yt[:], in0=yt[:], scalar=gn[:, kc:kc + 1], in1=st[:], op0=ALU.mult, op1=ALU.mult),
             reads=[yb, sb, gnb], writes=[yb])
        P.dma("sp", gv[kc], yt[:], reads=[yb], writes=[yg_b], key=yb)
    gemm(C, K=VW, w_ap=w_out, slabs=list(range(D // 512)), at_dram=ygT, at_dram_b=yg_b, mode_of=fm,
         epilogue=ep_store_f32(C, yo, yo_b), kbs=8)
    phase_c(C, yo, yo_b, xT, None, xs, xs_b, gg1, tbb, rstd, rstdb)
    mlp_block(C, xs, xs_b, x1T, x1_b, rstd, rstdb, wc2, tb[:, 3, :], gg2, tbb, hT, hTb, w1, w2, aT, aT_b, yo, yo_b, True)
    phase_a(C, x1T, x1_b, rstd, rstdb, wc1n, tb[:, 8, :], tbb, hT, hTb, do_stats=False)
    n = 0
    for g in range(8):
        for tt in range(8):
            for ri in range(2):
                bank = n % 8
                n += 1
                for kc in range(4):
                    P.op("pe", lambda e, bank=bank, g=g, kc=kc, tt=tt, ri=ri: e.matmul(C.ps[bank][:], lhsT=hT[:, g * 4 + kc, tt * 128:(tt + 1) * 128],
                                                                                    rhs=cs[:, kc, ri * 512:(ri + 1) * 512], start=(kc == 0), stop=(kc == 3)),
                         reads=[hTb[g * 4 + kc], csb], writes=[C.psb[bank]])
                st, sb = C.st16.next()
                if n % 2 == 0:
                    P.op("act", lambda e, st=st, bank=bank: e.activation(out=st[:], in_=C.ps[bank][:], func=AF.Copy), reads=[C.psb[bank]], writes=[sb])
                else:
                    P.op("dve", lambda e, st=st, bank=bank: e.tensor_copy(out=st[:], in_=C.ps[bank][:]), reads=[C.psb[bank]], writes=[sb])
                P.dma("sp", Z[g, ri, tt * 128:(tt + 1) * 128, :], st[:], reads=[sb], writes=[z_b], key=sb)
    P.emit(final_waits=C.st16.b + C.st32.b + C.f32r.b + C.bf1k.b)
    return nc


def run_l4(x, yT_all, sg_parts, mod, norm_g, gn_g, w_out, w1, w2):
    tabs = np.stack([col_layout(mod[0, 2]), col_layout(norm_g[0, 1]), col_layout(mod[0, 4]), col_layout(mod[0, 3]),
                     col_layout(norm_g[0, 2]), col_layout(mod[0, 5]), col_layout(norm_g[0, 3]),
                     col_layout(mod[1, 1]), col_layout(mod[1, 0]), col_layout(norm_g[1, 0])], axis=1)
    gng = col_layout(gn_g)
    cs = dft_consts()
    in_maps = []
    for i in range(NCORE):
        tok = slice(i * T, (i + 1) * T)
        in_maps.append({"xT": np.ascontiguousarray(x[tok].T), "yT": np.ascontiguousarray(yT_all[:, tok]), "sgT": sg_parts[i],
                        "gng": gng, "tabs": np.ascontiguousarray(tabs), "w_out": w_out, "w1": w1, "w2": w2, "cs": cs})
    nc = build_l4()
    res = run_bass_kernel_spmd(nc, in_maps, core_ids=list(range(NCORE)))
    return res.results


def fft_consts():
    a = np.arange(128)
    ang = 2 * np.pi * np.outer(a, a) / 128.0
    f128 = np.stack([np.cos(ang), np.sin(ang), -np.sin(ang)], axis=1).astype(np.float32).astype(NPBF)
    n1 = np.arange(128)[:, None]
    m2 = np.arange(64)[None, :]
    th = 2 * np.pi * (n1 * m2) / 8192.0
    tw = np.stack([np.cos(th), np.sin(th)], axis=1).astype(np.float32)
    b = np.arange(64)
    ang64 = 2 * np.pi * np.outer(b, b) / 64.0
    f64 = np.stack([np.cos(ang64), np.sin(ang64)], axis=1).astype(np.float32).astype(NPBF)
    return np.ascontiguousarray(f128), np.ascontiguousarray(tw), np.ascontiguousarray(f64)


def build_l5():
    nc = bass.Bass("TRN2", target_bir_lowering=False)
    Zr = nc.dram_tensor("Zr", [SEQ, 512], BF16, kind="ExternalInput").ap()
    Zi = nc.dram_tensor("Zi", [SEQ, 512], BF16, kind="ExternalInput").ap()
    f128_d = nc.dram_tensor("f128", [128, 3, 128], BF16, kind="ExternalInput").ap()
    tw_d = nc.dram_tensor("tw", [128, 2, 64], F32, kind="ExternalInput").ap()
    f64_d = nc.dram_tensor("f64", [64, 2, 64], BF16, kind="ExternalInput").ap()
    mixT = nc.dram_tensor("mixT", [512, SEQ], BF16, kind="ExternalOutput").ap()
    Yp = nc.dram_tensor("Yp", [2, 128, 64, 512], BF16).ap()
    P = Prog(nc)
    fw = emit_fft(P, Zr, Zi, f128_d, tw_d, f64_d, mixT, Yp)
    P.emit(final_waits=fw)
    return nc


def emit_fft(P, Zr, Zi, f128_d, tw_d, f64_d, mixT, Yp, in_bufs=(), out_buf=None):
    in_bufs = list(in_bufs)
    banks = [P.psum("fb%d" % i, [128, 512], F32) for i in range(8)]
    bb = P.bufs("fb", 8)

    def tab(name, shape, dt, src):
        t = P.sbuf(name, shape, dt)
        b = Buf(name)
        P.dma("sp", t[:], src, writes=[b], key=b)
        return t, b
    f128, f128b = tab("f_f128", [128, 3, 128], BF16, f128_d)
    tw, twb = tab("f_tw", [128, 2, 64], F32, tw_d)
    f64, f64b = tab("f_f64", [64, 2, 64], BF16, f64_d)
    G = 8
    xr_ring = Ring(P, "f_xr", 2, [128, G, 512], BF16)
    xi_ring = Ring(P, "f_xi", 2, [128, G, 512], BF16)
    yr_ring = Ring(P, "f_yr", 2, [128, G, 512], BF16)
    yi_ring = Ring(P, "f_yi", 2, [128, G, 512], BF16)
    t_ring = Ring(P, "f_t", 4, [128, 512], F32)
    zr_v = Zr.rearrange("(m1 m2) f -> m1 m2 f", m2=64)
    zi_v = Zi.rearrange("(m1 m2) f -> m1 m2 f", m2=64)
    yp_b = Buf("Yp")
    n = 0
    for gb in range(64 // G):
        xr, xrb = xr_ring.next()
        xi, xib = xi_ring.next()
        P.dma("sp", xr[:], zr_v[:, gb * G:(gb + 1) * G, :], reads=in_bufs, writes=[xrb], key=xrb)
        P.dma("sp", xi[:], zi_v[:, gb * G:(gb + 1) * G, :], reads=in_bufs, writes=[xib], key=xib)
        yr, yrb = yr_ring.next()
        yi, yib = yi_ring.next()
        for j in range(G):
            m2 = gb * G + j
            bA, bB = (n % 4) * 2, (n % 4) * 2 + 1
            n += 1
            P.op("pe", lambda e, bA=bA, xr=xr, j=j: e.matmul(banks[bA][:], lhsT=f128[:, 0, :], rhs=xr[:, j, :], start=True, stop=False), reads=[f128b, xrb], writes=[bb[bA]])
            P.op("pe", lambda e, bA=bA, xi=xi, j=j: e.matmul(banks[bA][:], lhsT=f128[:, 1, :], rhs=xi[:, j, :], start=False, stop=True), reads=[f128b, xib], writes=[bb[bA]])
            P.op("pe", lambda e, bB=bB, xi=xi, j=j: e.matmul(banks[bB][:], lhsT=f128[:, 0, :], rhs=xi[:, j, :], start=True, stop=False), reads=[f128b, xib], writes=[bb[bB]])
            P.op("pe", lambda e, bB=bB, xr=xr, j=j: e.matmul(banks[bB][:], lhsT=f128[:, 2, :], rhs=xr[:, j, :], start=False, stop=True), reads=[f128b, xrb], writes=[bb[bB]])
            c_ = tw[:, 0, m2:m2 + 1]
            s_ = tw[:, 1, m2:m2 + 1]
            t1, t1b = t_ring.next()
            t2, t2b = t_ring.next()
            P.op("act", lambda e, t1=t1, bB=bB, s_=s_: e.activation(out=t1[:], in_=banks[bB][:], func=AF.Copy, scale=s_), reads=[bb[bB], twb], writes=[t1b])
            P.op("act", lambda e, t2=t2, bA=bA, s_=s_: e.activation(out=t2[:], in_=banks[bA][:], func=AF.Copy, scale=s_), reads=[bb[bA], twb], writes=[t2b])
            P.op("dve", lambda e, yr=yr, j=j, bA=bA, c_=c_, t1=t1: e.scalar_tensor_tensor(out=yr[:, j, :], in0=banks[bA][:], scalar=c_, in1=t1[:], op0=ALU.mult, op1=ALU.add),
                 reads=[bb[bA], twb, t1b, t2b], writes=[yrb])
            P.op("dve", lambda e, yi=yi, j=j, bB=bB, c_=c_, t2=t2: e.scalar_tensor_tensor(out=yi[:, j, :], in0=banks[bB][:], scalar=c_, in1=t2[:], op0=ALU.mult, op1=ALU.subtract),
                 reads=[bb[bB], twb, t2b], writes=[yib])
        P.dma("sp", Yp[0, :, gb * G:(gb + 1) * G, :], yr[:], reads=[yrb], writes=[yp_b], key=yrb)
        P.dma("sp", Yp[1, :, gb * G:(gb + 1) * G, :], yi[:], reads=[yib], writes=[yp_b], key=yib)
    NB = 16
    ar_ring = Ring(P, "f_ar", 2, [64, NB, 512], BF16)
    ai_ring = Ring(P, "f_ai", 2, [64, NB, 512], BF16)
    mt = [P.sbuf("f_mt%d" % fc, [128, 64, 128], BF16) for fc in range(4)]
    mtb = P.bufs("f_mt", 4)
    ypr = Yp[0].rearrange("n1 m2 f -> m2 n1 f")
    ypi = Yp[1].rearrange("n1 m2 f -> m2 n1 f")
    for nb in range(128 // NB):
        ar, arb = ar_ring.next()
        ai, aib = ai_ring.next()
        P.dma("sp", ar[:], ypr[:, nb * NB:(nb + 1) * NB, :], reads=[yp_b], writes=[arb], key=arb)
        P.dma("sp", ai[:], ypi[:, nb * NB:(nb + 1) * NB, :], reads=[yp_b], writes=[aib], key=aib)
        for half in range(NB // 8):
            for fc in range(4):
                bank = (half % 2) * 4 + fc
                for j in range(8):
                    jj = half * 8 + j
                    P.op("pe", lambda e, bank=bank, ar=ar, jj=jj, fc=fc, j=j: e.matmul(banks[bank][:, j * 64:(j + 1) * 64], lhsT=ar[:, jj, fc * 128:(fc + 1) * 128],
                                                                                  rhs=f64[:, 0, :], start=True, stop=False),
                         reads=[arb, f64b], writes=[bb[bank]])
                    P.op("pe", lambda e, bank=bank, ai=ai, jj=jj, fc=fc, j=j: e.matmul(banks[bank][:, j * 64:(j + 1) * 64], lhsT=ai[:, jj, fc * 128:(fc + 1) * 128],
                                                                                  rhs=f64[:, 1, :], start=False, stop=True),
                         reads=[aib, f64b], writes=[bb[bank]])
                n1b = nb * NB + half * 8
                src = banks[bank][:].rearrange("p (j n2) -> p j n2", j=8)
                dst = mt[fc][:, :, n1b:n1b + 8].rearrange("p n2 j -> p j n2")
                if fc % 2 == 0:
                    P.op("act", lambda e, dst=dst, src=src: e.activation(out=dst, in_=src, func=AF.Copy), reads=[bb[bank]], writes=[mtb[fc]])
                else:
                    P.op("dve", lambda e, dst=dst, src=src: e.tensor_copy(out=dst, in_=src), reads=[bb[bank]], writes=[mtb[fc]])
    for fc in range(4):
        P.dma("sp", mixT[fc * 128:(fc + 1) * 128, :], mt[fc][:].rearrange("p a b -> p (a b)"), reads=[mtb[fc]],
              writes=[out_buf] if out_buf is not None else [], key=mtb[fc])
    return list(mtb)


def run_l5(Zr_g, Zi_g):
    f128, tw, f64 = fft_consts()
    in_maps = [{"Zr": np.ascontiguousarray(Zr_g[i]), "Zi": np.ascontiguousarray(Zi_g[i]), "f128": f128, "tw": tw, "f64": f64} for i in range(NCORE)]
    nc = build_l5()
    res = run_bass_kernel_spmd(nc, in_maps, core_ids=list(range(NCORE)))
    return np.concatenate([res.results[i]["mixT"] for i in range(NCORE)], axis=0)


def build_l6():
    nc = bass.Bass("TRN2", target_bir_lowering=False)
    x1T = nc.dram_tensor("x1T", [D, T], F32, kind="ExternalInput").ap()
    mixT = nc.dram_tensor("mixT", [D, T], BF16, kind="ExternalInput").ap()
    tabs = nc.dram_tensor("tabs", [128, 8, KC], F32, kind="ExternalInput").ap()
    fno_w = nc.dram_tensor("fno_w", [D, D], F32, kind="ExternalInput").ap()
    w1 = nc.dram_tensor("w1", [D, DFF], F32, kind="ExternalInput").ap()
    w2 = nc.dram_tensor("w2", [DFF, D], F32, kind="ExternalInput").ap()
    xoT = nc.dram_tensor("xoT", [D, T], F32, kind="ExternalOutput").ap()
    yo = nc.dram_tensor("yo", [D, T], F32).ap()
    xs = nc.dram_tensor("xs", [D, T], F32).ap()
    aT = nc.dram_tensor("aT", [DFF, T], BF16).ap()
    P = Prog(nc)
    C = Ctx(P)
    setup_eps(C)
    tb, tbb = load_tab(C, "tabs", tabs, [128, 8, KC])
    gg1 = tab_prod(C, tb, tbb, "gg1", 0, 1, False)
    wc2 = tab_prod(C, tb, tbb, "wc2", 2, 4, True)
    gg2 = tab_prod(C, tb, tbb, "gg2", 5, 6, False)
    hT = P.sbuf("hT", [128, KC, T], BF16)
    hTb = P.bufs("hT", KC)
    rstd = P.sbuf("rstd", [128, T], F32)
    rstdb = Buf("rstd")
    yo_b, xs_b, aT_b, xo_b = Buf("yo"), Buf("xs"), Buf("aT"), Buf("xoT")
    gemm(C, K=D, w_ap=fno_w, slabs=list(range(D // 512)), at_dram=mixT, at_dram_b=None, mode_of=fm,
         epilogue=ep_store_f32(C, yo, yo_b), kbs=8)
    phase_c(C, yo, yo_b, x1T, None, xs, xs_b, gg1, tbb, rstd, rstdb, fbcol=tb[:, 7, :])
    mlp_block(C, xs, xs_b, xoT, xo_b, rstd, rstdb, wc2, tb[:, 3, :], gg2, tbb, hT, hTb, w1, w2, aT, aT_b, yo, yo_b, False)
    P.emit(final_waits=C.st16.b + C.st32.b + C.f32r.b + C.bf1k.b)
    return nc


def run_l6(x1T_parts, mixT_all, mod, norm_g, fno_b, fno_w, w1, w2):
    tabs = np.stack([col_layout(mod[1, 2]), col_layout(norm_g[1, 1]), col_layout(mod[1, 4]), col_layout(mod[1, 3]),
                     col_layout(norm_g[1, 2]), col_layout(mod[1, 5]), col_layout(norm_g[1, 3]), col_layout(fno_b)], axis=1)
    in_maps = []
    for i in range(NCORE):
        tok = slice(i * T, (i + 1) * T)
        in_maps.append({"x1T": x1T_parts[i], "mixT": np.ascontiguousarray(mixT_all[:, tok]), "tabs": np.ascontiguousarray(tabs),
                        "fno_w": fno_w, "w1": w1, "w2": w2})
    nc = build_l6()
    res = run_bass_kernel_spmd(nc, in_maps, core_ids=list(range(NCORE)))
    return res.results


def kernel(x, c, ada_w, ada_b, norm_g, ret_w_in, ret_w_out, ret_gn_g, ret_decay_fwd, ret_decay_bwd,
           fno_w, fno_b, mlp_w1, mlp_w2):
    x = np.asarray(x, np.float32)[0]
    mod = run_l1(np.asarray(c, np.float32)[0], np.asarray(ada_w, np.float32), np.asarray(ada_b, np.float32))
    norm_g = np.asarray(norm_g, np.float32)
    r2 = run_l2(x, mod, norm_g, np.asarray(ret_w_in, np.float32)[0])
    qT = np.concatenate([r2[i]["qT"] for i in range(NCORE)], axis=1).reshape(HEADS, DK, SEQ)
    kT = np.concatenate([r2[i]["kT"] for i in range(NCORE)], axis=1).reshape(HEADS, DK, SEQ)
    v = np.concatenate([r2[i]["v"] for i in range(NCORE)], axis=0)
    sg = [r2[i]["sgT"] for i in range(NCORE)]
    del r2
    yT = run_l3(qT, kT, v, np.asarray(ret_decay_fwd, np.float32)[0], np.asarray(ret_decay_bwd, np.float32)[0])
    del qT, kT, v
    r4 = run_l4(x, yT, sg, mod, norm_g, np.asarray(ret_gn_g, np.float32)[0], np.asarray(ret_w_out, np.float32)[0],
                np.asarray(mlp_w1, np.float32)[0], np.asarray(mlp_w2, np.float32)[0])
    Z = np.concatenate([r4[i]["Z"] for i in range(NCORE)], axis=2)
    x1 = [r4[i]["x1T"] for i in range(NCORE)]
    del r4
    mixT = run_l5(Z[:, 0], Z[:, 1])
    r6 = run_l6(x1, mixT, mod, norm_g, np.asarray(fno_b, np.float32)[0], np.asarray(fno_w, np.float32)[0],
                np.asarray(mlp_w1, np.float32)[1], np.asarray(mlp_w2, np.float32)[1])
    out = np.concatenate([r6[i]["xoT"].T for i in range(NCORE)], axis=0)
    return np.ascontiguousarray(out[None]).astype(np.float32)
```

```python
import contextlib
import numpy as np
import ml_dtypes
import concourse.bass as bass
import concourse.mybir as mybir
from concourse.bass_utils import run_bass_kernel_spmd

F32 = mybir.dt.float32
BF16 = mybir.dt.bfloat16
AF = mybir.ActivationFunctionType
ALU = mybir.AluOpType
NPBF = ml_dtypes.bfloat16

D = 4096
SEQ = 8192
NCORE = 8
T = SEQ // NCORE
KC = D // 128
HEADS = 16
DK = 256
DV = 512
VW = HEADS * DV
DFF = 4 * D
EPS = 1e-6
ROPE_BASE = 10000.0


class Buf:
    __slots__ = ("name", "w", "r", "sem", "tot")

    def __init__(self, name):
        self.name = name
        self.w = []
        self.r = {}
        self.sem = None
        self.tot = 0


class Ins:
    __slots__ = ("eng", "fn", "deps", "isdma", "sem", "val", "needed")

    def __init__(self, eng, fn, isdma):
        self.eng = eng
        self.fn = fn
        self.isdma = isdma
        self.deps = []
        self.sem = None
        self.val = 0
        self.needed = False


class Prog:
    ENGS = ("pe", "act", "dve", "pool", "sp")

    def __init__(self, nc):
        self.nc = nc
        self.stack = contextlib.ExitStack()
        self.streams = {e: [] for e in self.ENGS}
        self.nsem = 0
        self.prog_sem = {e: self.sem("prog_" + e) for e in self.ENGS}
        self.n_t = 0

    def sem(self, name):
        self.nsem += 1
        return self.stack.enter_context(self.nc.semaphore(name))

    def sbuf(self, name, shape, dt):
        return self.stack.enter_context(self.nc.sbuf_tensor(name, list(shape), dt))

    def psum(self, name, shape, dt):
        return self.stack.enter_context(self.nc.psum_tensor(name, list(shape), dt))

    def bufs(self, name, n):
        return [Buf("%s%d" % (name, i)) for i in range(n)]

    def _track(self, ins, reads, writes):
        deps = {}
        grp = {}
        for b in reads:
            for w in b.w:
                deps[id(w)] = w
        for b in writes:
            g = bool(ins.isdma and b.w and not b.r and all(w.isdma for w in b.w))
            grp[id(b)] = g
            if not g:
                for w in b.w:
                    deps[id(w)] = w
            for r in b.r.values():
                deps[id(r)] = r
        deps.pop(id(ins), None)
        for b in reads:
            b.r[id(ins) if ins.isdma else ins.eng] = ins
        for b in writes:
            if grp[id(b)]:
                b.w.append(ins)
            else:
                b.w = [ins]
                b.r = {}
        for d in deps.values():
            if d.eng == "pe" and ins.eng == "pe" and not d.isdma and not ins.isdma:
                continue
            d.needed = True
            ins.deps.append(d)

    def op(self, eng, fn, reads=(), writes=()):
        ins = Ins(eng, fn, False)
        self._track(ins, reads, writes)
        self.streams[eng].append(ins)
        return ins

    def dma(self, queue, out, in_, reads=(), writes=(), key=None, **kw):
        ins = Ins(queue, lambda e: e.dma_start(out=out, in_=in_, **kw), True)
        if key.sem is None:
            key.sem = self.sem("d_" + key.name)
        key.tot += 16
        ins.sem = key.sem
        ins.val = key.tot
        self._track(ins, reads, writes)
        self.streams[queue].append(ins)
        return ins

    def emit(self, final_waits=()):
        nc = self.nc
        for e in self.ENGS:
            c = 0
            for ins in self.streams[e]:
                if not ins.isdma and ins.needed:
                    c += 1
                    ins.sem = self.prog_sem[e]
                    ins.val = c
        handles = {"pe": "tensor", "act": "scalar", "dve": "vector", "pool": "gpsimd", "sp": "sync"}

        def run(ename, eng):
            known = {}
            for ins in self.streams[ename]:
                need = {}
                for d in ins.deps:
                    k = id(d.sem)
                    if k not in need or need[k][1] < d.val:
                        need[k] = (d.sem, d.val)
                for k, (s, v) in need.items():
                    if known.get(k, 0) >= v:
                        continue
                    eng.wait_ge(s, v)
                    known[k] = v
                r = ins.fn(eng)
                if ins.isdma:
                    r.then_inc(ins.sem, 16)
                elif ins.needed:
                    r.then_inc(ins.sem, 1)
            if ename == "sp":
                for b in final_waits:
                    if b.sem is not None:
                        eng.wait_ge(b.sem, b.tot)

        with nc.Block() as block:
            @block.tensor
            def _(e):
                run("pe", e)

            @block.scalar
            def _(e):
                run("act", e)

            @block.vector
            def _(e):
                run("dve", e)

            @block.gpsimd
            def _(e):
                run("pool", e)

            @block.sync
            def _(e):
                run("sp", e)
        self.stack.close()


class Ring:
    def __init__(self, P, name, n, shape, dt):
        self.t = [P.sbuf("%s%d" % (name, i), shape, dt) for i in range(n)]
        self.b = P.bufs(name, n)
        self.i = 0
        self.n = n

    def next(self):
        k = self.i % self.n
        self.i += 1
        return self.t[k], self.b[k]


class Ctx:
    def __init__(self, P):
        self.P = P
        self.ps = [P.psum("ps%d" % i, [128, 512], F32) for i in range(8)]
        self.psb = P.bufs("psb", 8)
        self.ones = P.sbuf("ones", [128, 128], BF16)
        self.ones_b = Buf("ones")
        P.op("pool", lambda e: e.memset(self.ones[:], 1.0), writes=[self.ones_b])
        self.ones32 = P.sbuf("ones32", [128, 128], F32)
        P.op("pool", lambda e: e.memset(self.ones32[:], 1.0), writes=[self.ones_b])
        self.acc = [P.sbuf("acc%d" % h, [128, 512], F32) for h in range(2)]
        self.acc_b = P.bufs("acc", 2)
        self.sq32 = Ring(P, "sq32", 2, [128, 512], F32)
        self.f32r = Ring(P, "f32r", 4, [128, 1024], F32)
        self.bf1k = Ring(P, "bf1k", 3, [128, 1024], BF16)
        self.st16 = Ring(P, "st16", 6, [128, 512], BF16)
        self.st32 = Ring(P, "st32", 4, [128, 512], F32)
        self.wr = None
        self.atr = None

    def wring(self):
        if self.wr is None:
            self.wr = Ring(self.P, "wsl", 3, [128, 16, 512], BF16)
        return self.wr

    def atring(self):
        if self.atr is None:
            self.atr = Ring(self.P, "atb", 2, [128, 8, T], BF16)
        return self.atr


def stats_rstd(C, src_tiles, nkc, out_rstd, out_b, nfeat, bias_col=None):
    P = C.P
    for kc in range(nkc):
        xt, xb = src_tiles(kc)
        sq, sqb = C.bf1k.next()
        if bias_col is not None:
            bc = bias_col(kc)
            P.op("act", lambda e, sq=sq, xt=xt, bc=bc: e.activation(out=sq[:], in_=xt, func=AF.Square, bias=bc),
                 reads=[xb], writes=[sqb])
        else:
            P.op("act", lambda e, sq=sq, xt=xt: e.activation(out=sq[:], in_=xt, func=AF.Square),
                 reads=[xb], writes=[sqb])
        for h in range(2):
            P.op("pe", lambda e, h=h, sq=sq, kc=kc: e.matmul(C.ps[6 + h][:], lhsT=C.ones[:], rhs=sq[:, h * 512:(h + 1) * 512],
                                                          start=(kc == 0), stop=(kc == nkc - 1)),
                 reads=[sqb, C.ones_b], writes=[C.psb[6 + h]])
    for h in range(2):
        sl = slice(h * 512, (h + 1) * 512)
        P.op("act", lambda e, h=h, sl=sl: e.activation(out=out_rstd[:, sl], in_=C.ps[6 + h][:], func=AF.Sqrt,
                                                      bias=C.eps_t[:, 0:1], scale=1.0 / nfeat),
             reads=[C.psb[6 + h], C.eps_b], writes=[out_b])
    P.op("dve", lambda e: e.reciprocal(out=out_rstd[:], in_=out_rstd[:]), reads=[out_b], writes=[out_b])


def gemm(C, *, K, w_ap, slabs, at_res=None, at_res_b=None, at_dram=None, at_dram_b=None, mode_of, epilogue, kbs=16):
    P = C.P
    nkc = K // 128
    nkb = nkc // kbs
    hk = kbs // 2
    wv = w_ap.rearrange("(kc p) n -> p kc n", p=128)
    wr = C.wring()
    plan = [(s, kb) for s in slabs for kb in range(nkb)]
    loaded = {}

    def load(i):
        s, kb = plan[i]
        wt, wb = wr.next()
        for q in range(2):
            P.dma("pool", wt[:, q * hk:(q + 1) * hk, :], wv[:, kb * kbs + q * hk: kb * kbs + (q + 1) * hk, s * 512:(s + 1) * 512],
                  writes=[wb], key=wb)
        at = None
        if at_dram is not None:
            att, atb = C.atring().next()
            av = at_dram.rearrange("(kc p) t -> p kc t", p=128)
            for q in range(2):
                P.dma("sp", att[:, q * hk:(q + 1) * hk, :], av[:, kb * kbs + q * hk: kb * kbs + (q + 1) * hk, :],
                      reads=[at_dram_b] if at_dram_b is not None else [], writes=[atb], key=atb)
            at = (att, atb)
        loaded[i] = (wt, wb, at)

    load(0)
    for i, (s, kb) in enumerate(plan):
        if i + 1 < len(plan):
            load(i + 1)
        wt, wb, at = loaded.pop(i)
        mode = mode_of(s)
        for tile in range(8):
            for k16 in range(kbs):
                kc = kb * kbs + k16
                if at is not None:
                    a_t, a_b = at[0], at[1]
                    akc = k16
                else:
                    a_t, a_b = at_res, at_res_b
                    akc = kc
                first = (kc == 0)
                last = (kc == nkc - 1)
                if mode == "fm":
                    nch, half = tile // 2, tile % 2
                    P.op("pe", lambda e, tile=tile, wt=wt, k16=k16, nch=nch, a_t=a_t, akc=akc, half=half, first=first, last=last:
                         e.matmul(C.ps[tile][:], lhsT=wt[:, k16, nch * 128:(nch + 1) * 128],
                                  rhs=a_t[:, akc, half * 512:(half + 1) * 512], start=first, stop=last),
                         reads=[wb, a_b[akc] if isinstance(a_b, list) else a_b], writes=[C.psb[tile]])
                else:
                    P.op("pe", lambda e, tile=tile, wt=wt, k16=k16, a_t=a_t, akc=akc, first=first, last=last:
                         e.matmul(C.ps[tile][:], lhsT=a_t[:, akc, tile * 128:(tile + 1) * 128],
                                  rhs=wt[:, k16, :], start=first, stop=last),
                         reads=[wb, a_b[akc] if isinstance(a_b, list) else a_b], writes=[C.psb[tile]])
            if kb == nkb - 1:
                epilogue(s, tile)


def load_tile(C, dram_ap, dram_b, dt=F32):
    ring = C.f32r if dt == F32 else C.bf1k
    t, b = ring.next()
    C.P.dma("sp", t[:], dram_ap, reads=[dram_b] if dram_b is not None else [], writes=[b], key=b)
    return t, b


def phase_a(C, xT_ap, x_b, rstd, rstd_b, wcol, bcol, tab_b, hT, hT_b, do_stats=True):
    P = C.P
    xv = xT_ap.rearrange("(kc p) t -> kc p t", p=128)
    if do_stats:
        stats_rstd(C, lambda kc: (lambda tb: (tb[0][:], tb[1]))(load_tile(C, xv[kc], x_b)), KC, rstd, rstd_b, D)
    for kc in range(KC):
        xt, xb = load_tile(C, xv[kc], x_b)
        P.op("dve", lambda e, xt=xt, kc=kc: e.scalar_tensor_tensor(out=xt[:], in0=xt[:], scalar=wcol[:, kc:kc + 1], in1=rstd[:],
                                                                  op0=ALU.mult, op1=ALU.mult),
             reads=[xb, rstd_b, tab_b], writes=[xb])
        P.op("act", lambda e, xt=xt, kc=kc: e.activation(out=hT[:, kc, :], in_=xt[:], func=AF.Identity, bias=bcol[:, kc:kc + 1]),
             reads=[xb, tab_b], writes=[hT_b[kc]])


def phase_c(C, yo_ap, yo_b, xT_ap, x_b, xo_ap, xo_b, ggcol, tab_b, rstd, rstd_b, fbcol=None, nstats=True, from_acc=False):
    P = C.P
    yv = yo_ap.rearrange("(kc p) t -> kc p t", p=128)
    xv = xT_ap.rearrange("(kc p) t -> kc p t", p=128)
    ov = xo_ap.rearrange("(kc p) t -> kc p t", p=128)
    ry, ryb = C.rstd_y, C.rstd_y_b
    if from_acc:
        rstd_from_acc(C, ry, ryb, D)
    else:
        stats_rstd(C, lambda kc: (lambda tb: (tb[0][:], tb[1]))(load_tile(C, yv[kc], yo_b)), KC, ry, ryb, D,
                   bias_col=(None if fbcol is None else (lambda kc: fbcol[:, kc:kc + 1])))
    for kc in range(KC):
        yt, yb = load_tile(C, yv[kc], yo_b)
        xt, xb = load_tile(C, xv[kc], x_b)
        if fbcol is None:
            P.op("dve", lambda e, yt=yt, kc=kc: e.scalar_tensor_tensor(out=yt[:], in0=yt[:], scalar=ggcol[:, kc:kc + 1], in1=ry[:],
                                                                      op0=ALU.mult, op1=ALU.mult),
                 reads=[yb, ryb, tab_b], writes=[yb])
        else:
            P.op("dve", lambda e, yt=yt, kc=kc: e.tensor_scalar(out=yt[:], in0=yt[:], scalar1=fbcol[:, kc:kc + 1], scalar2=ggcol[:, kc:kc + 1],
                                                               op0=ALU.add, op1=ALU.mult),
                 reads=[yb, tab_b], writes=[yb])
            P.op("dve", lambda e, yt=yt: e.tensor_tensor(out=yt[:], in0=yt[:], in1=ry[:], op=ALU.mult),
                 reads=[yb, ryb], writes=[yb])
        P.op("dve", lambda e, yt=yt, xt=xt: e.tensor_tensor(out=xt[:], in0=xt[:], in1=yt[:], op=ALU.add),
             reads=[yb, xb], writes=[xb])
        P.dma("sp", ov[kc], xt[:], reads=[xb], writes=[xo_b], key=xb)
        if nstats:
            sq, sqb = C.bf1k.next()
            P.op("act", lambda e, sq=sq, xt=xt: e.activation(out=sq[:], in_=xt[:], func=AF.Square), reads=[xb], writes=[sqb])
            for h in range(2):
                P.op("pe", lambda e, h=h, sq=sq, kc=kc: e.matmul(C.ps[6 + h][:], lhsT=C.ones[:], rhs=sq[:, h * 512:(h + 1) * 512],
                                                              start=(kc == 0), stop=(kc == KC - 1)),
                     reads=[sqb, C.ones_b], writes=[C.psb[6 + h]])
    if nstats:
        for h in range(2):
            sl = slice(h * 512, (h + 1) * 512)
            P.op("act", lambda e, h=h, sl=sl: e.activation(out=rstd[:, sl], in_=C.ps[6 + h][:], func=AF.Sqrt,
                                                          bias=C.eps_t[:, 0:1], scale=1.0 / D),
                 reads=[C.psb[6 + h], C.eps_b], writes=[rstd_b])
        P.op("dve", lambda e: e.reciprocal(out=rstd[:], in_=rstd[:]), reads=[rstd_b], writes=[rstd_b])


def ep_store_f32(C, out_ap, out_b, fbcol=None, tab_b=None):
    P = C.P
    cnt = [0]
    for h in range(2):
        P.op("pool", lambda e, h=h: e.memset(C.acc[h][:], 0.0), writes=[C.acc_b[h]])

    def ep(s, tile):
        nch, half = tile // 2, tile % 2
        st, sb = C.st32.next()
        sq, sqb = C.sq32.next()
        eng = "act" if cnt[0] % 2 == 0 else "dve"
        cnt[0] += 1
        if eng == "act":
            P.op("act", lambda e: e.activation(out=st[:], in_=C.ps[tile][:], func=AF.Copy),
                 reads=[C.psb[tile]], writes=[sb])
        else:
            P.op("dve", lambda e: e.tensor_copy(out=st[:], in_=C.ps[tile][:]),
                 reads=[C.psb[tile]], writes=[sb])
        if fbcol is None:
            P.op("act", lambda e: e.activation(out=sq[:], in_=st[:], func=AF.Square),
                 reads=[sb], writes=[sqb])
        else:
            kc = s * 4 + nch
            P.op("act", lambda e: e.activation(out=sq[:], in_=st[:], func=AF.Square, bias=fbcol[:, kc:kc + 1]),
                 reads=[sb, tab_b], writes=[sqb])
        P.op("dve", lambda e: e.tensor_tensor(out=C.acc[half][:], in0=C.acc[half][:], in1=sq[:], op=ALU.add),
             reads=[sqb, C.acc_b[half]], writes=[C.acc_b[half]])
        r0 = s * 512 + nch * 128
        P.dma("sp", out_ap[r0:r0 + 128, half * 512:(half + 1) * 512], st[:], reads=[sb], writes=[out_b], key=sb)
    return ep


def rstd_from_acc(C, out_rstd, out_b, nfeat):
    P = C.P
    for h in range(2):
        sl = slice(h * 512, (h + 1) * 512)
        P.op("pe", lambda e, h=h: e.matmul(C.ps[6 + h][:], lhsT=C.ones32[:], rhs=C.acc[h][:], start=True, stop=True),
             reads=[C.acc_b[h], C.ones_b], writes=[C.psb[6 + h]])
        P.op("act", lambda e, h=h, sl=sl: e.activation(out=out_rstd[:, sl], in_=C.ps[6 + h][:], func=AF.Sqrt,
                                                      bias=C.eps_t[:, 0:1], scale=1.0 / nfeat),
             reads=[C.psb[6 + h], C.eps_b], writes=[out_b])
    P.op("dve", lambda e: e.reciprocal(out=out_rstd[:], in_=out_rstd[:]), reads=[out_b], writes=[out_b])


def ep_act_bf16(C, out_ap, out_b, func, row_off=0, square_after=False):
    P = C.P

    def ep(s, tile):
        nch, half = tile // 2, tile % 2
        st, sb = C.st16.next()
        if square_after:
            tt, tb = C.st32.next()
            P.op("act", lambda e, tt=tt, tile=tile: e.activation(out=tt[:], in_=C.ps[tile][:], func=func),
                 reads=[C.psb[tile]], writes=[tb])
            P.op("dve", lambda e, st=st, tt=tt: e.tensor_tensor(out=st[:], in0=tt[:], in1=tt[:], op=ALU.mult),
                 reads=[tb], writes=[sb])
        else:
            P.op("act", lambda e, st=st, tile=tile: e.activation(out=st[:], in_=C.ps[tile][:], func=func),
                 reads=[C.psb[tile]], writes=[sb])
        r0 = s * 512 + nch * 128 - row_off
        P.dma("sp", out_ap[r0:r0 + 128, half * 512:(half + 1) * 512], st[:], reads=[sb], writes=[out_b], key=sb)
    return ep


def load_tab(C, name, dram_ap, shape):
    t = C.P.sbuf(name + "_sb", shape, F32)
    b = Buf(name)
    C.P.dma("sp", t[:], dram_ap, writes=[b], key=b)
    return t, b


def setup_eps(C):
    C.eps_t = C.P.sbuf("eps", [128, 1], F32)
    C.eps_b = Buf("eps")
    C.P.op("pool", lambda e: e.memset(C.eps_t[:], EPS), writes=[C.eps_b])
    C.rstd_y = C.P.sbuf("rstd_y", [128, T], F32)
    C.rstd_y_b = Buf("rstd_y")


NJ = 6 * D // 128 // NCORE


def build_l1():
    nc = bass.Bass("TRN2", target_bir_lowering=False)
    cT = nc.dram_tensor("cT", [128, KC], F32, kind="ExternalInput").ap()
    w = nc.dram_tensor("w", [2, D, NJ * 128], F32, kind="ExternalInput").ap()
    b = nc.dram_tensor("b", [128, 2 * NJ], F32, kind="ExternalInput").ap()
    out = nc.dram_tensor("mod", [128, 2 * NJ], F32, kind="ExternalOutput").ap()
    P = Prog(nc)
    ct = P.sbuf("ct", [128, KC], F32)
    ctb = Buf("ct")
    sg = P.sbuf("sg", [128, KC], BF16)
    bt = P.sbuf("bt", [128, 2 * NJ], F32)
    btb = Buf("bt")
    res = P.sbuf("res", [128, 2 * NJ], F32)
    resb = Buf("res")
    pm = P.psum("pm", [128, 2 * NJ], F32)
    pmb = Buf("pm")
    wr = Ring(P, "w", 4, [128, KC, 128], BF16)
    P.dma("sp", ct[:], cT, writes=[ctb], key=ctb)
    P.dma("sp", bt[:], b, writes=[btb], key=btb)
    P.op("act", lambda e: e.activation(out=sg[:], in_=ct[:], func=AF.Silu), reads=[ctb], writes=[ctb])
    for l in range(2):
        wv = w[l].rearrange("(kc p) n -> p kc n", p=128)
        for j in range(NJ):
            wt, wb = wr.next()
            P.dma("pool", wt[:], wv[:, :, j * 128:(j + 1) * 128], writes=[wb], key=wb)
            col = l * NJ + j
            for kc in range(KC):
                P.op("pe", lambda e, wt=wt, kc=kc, col=col: e.matmul(pm[:, col:col + 1], lhsT=wt[:, kc, :], rhs=sg[:, kc:kc + 1],
                                                                  start=(kc == 0), stop=(kc == KC - 1)),
                     reads=[wb, ctb], writes=[pmb])
    P.op("dve", lambda e: e.tensor_tensor(out=res[:], in0=pm[:], in1=bt[:], op=ALU.add), reads=[pmb, btb], writes=[resb])
    P.dma("sp", out, res[:], reads=[resb], key=resb)
    P.emit(final_waits=[resb])
    return nc


def run_l1(c, ada_w, ada_b):
    cT = np.ascontiguousarray(c.reshape(KC, 128).T)
    in_maps = []
    for i in range(NCORE):
        cols = slice(i * NJ * 128, (i + 1) * NJ * 128)
        w = np.ascontiguousarray(ada_w[:, :, cols])
        b = ada_b[:, cols].reshape(2, NJ, 128).transpose(2, 0, 1).reshape(128, 2 * NJ)
        in_maps.append({"cT": cT, "w": w, "b": np.ascontiguousarray(b)})
    nc = build_l1()
    res = run_bass_kernel_spmd(nc, in_maps, core_ids=list(range(NCORE)))
    mod = np.zeros((2, 6 * D), np.float32)
    for i in range(NCORE):
        m = res.results[i]["mod"].reshape(128, 2, NJ)
        mod[:, i * NJ * 128:(i + 1) * NJ * 128] = m.transpose(1, 2, 0).reshape(2, NJ * 128)
    return mod.reshape(2, 6, D)


def col_layout(v):
    v = np.asarray(v, np.float32)
    lead = v.shape[:-1]
    n = v.shape[-1] // 128
    r = v.reshape(lead + (n, 128))
    r = np.moveaxis(r, -1, 0)
    return np.ascontiguousarray(r)


def rope_tables():
    half = DK // 2
    inv = (np.float32(ROPE_BASE) ** (-np.arange(half, dtype=np.float32) / np.float32(half))).astype(np.float32)
    pos = np.arange(SEQ, dtype=np.float32)
    ang = (pos[:, None] * inv[None, :]).astype(np.float32)
    cos = np.cos(ang.astype(np.float64)).astype(np.float32).T
    sin = np.sin(ang.astype(np.float64)).astype(np.float32).T
    return np.ascontiguousarray(cos), np.ascontiguousarray(sin)


def build_l2():
    nc = bass.Bass("TRN2", target_bir_lowering=False)
    xT = nc.dram_tensor("xT", [D, T], F32, kind="ExternalInput").ap()
    tabs = nc.dram_tensor("tabs", [128, 3, KC], F32, kind="ExternalInput").ap()
    rope = nc.dram_tensor("rope", [128, 4, T], F32, kind="ExternalInput").ap()
    w_in = nc.dram_tensor("w_in", [D, 6 * D], F32, kind="ExternalInput").ap()
    qT = nc.dram_tensor("qT", [D, T], BF16, kind="ExternalOutput").ap()
    kT = nc.dram_tensor("kT", [D, T], BF16, kind="ExternalOutput").ap()
    v = nc.dram_tensor("v", [T, VW], BF16, kind="ExternalOutput").ap()
    sgT = nc.dram_tensor("sgT", [VW, T], BF16, kind="ExternalOutput").ap()
    P = Prog(nc)
    C = Ctx(P)
    setup_eps(C)
    tb, tbb = load_tab(C, "tabs", tabs, [128, 3, KC])
    rp, rpb = load_tab(C, "rope", rope, [128, 4, T])
    wcol = P.sbuf("wcol", [128, KC], F32)
    P.op("dve", lambda e: e.scalar_tensor_tensor(out=wcol[:], in0=tb[:, 0, :], scalar=1.0, in1=tb[:, 2, :], op0=ALU.add, op1=ALU.mult),
         reads=[tbb], writes=[tbb])
    hT = P.sbuf("hT", [128, KC, T], BF16)
    hTb = P.bufs("hT", KC)
    rstd = P.sbuf("rstd", [128, T], F32)
    rstdb = Buf("rstd")
    phase_a(C, xT, None, rstd, rstdb, wcol, tb[:, 1, :], tbb, hT, hTb)
    outb = Buf("outs")

    def mode_of(s):
        return "tm" if 16 <= s < 32 else "fm"

    ep_gate = ep_act_bf16(C, sgT, outb, AF.Silu, row_off=2 * D + VW)

    def ep(s, tile):
        if s < 16:
            nch, half = tile // 2, tile % 2
            if nch % 2 == 0:
                return
            t1, t2 = tile - 2, tile
            sl = slice(half * 512, (half + 1) * 512)
            ci, si = (0, 1) if s < 8 else (2, 3)
            dst = qT if s < 8 else kT
            r0 = (s % 8) * 512 + (nch - 1) * 128
            def rot(ta, tb_, op, dst_ap):
                a, ab = C.st32.next()
                b2, bb = C.st32.next()
                o, ob = C.st16.next()
                P.op("dve", lambda e: e.tensor_tensor(out=a[:], in0=C.ps[t1][:], in1=rp[:, ta, sl], op=ALU.mult), reads=[C.psb[t1], rpb], writes=[ab])
                P.op("dve", lambda e: e.tensor_tensor(out=b2[:], in0=C.ps[t2][:], in1=rp[:, tb_, sl], op=ALU.mult), reads=[C.psb[t2], rpb], writes=[bb])
                P.op("dve", lambda e: e.tensor_tensor(out=o[:], in0=a[:], in1=b2[:], op=op), reads=[ab, bb], writes=[ob])
                P.dma("sp", dst_ap, o[:], reads=[ob], writes=[outb], key=ob)
            rot(ci, si, ALU.subtract, dst[r0:r0 + 128, sl])
            rot(si, ci, ALU.add, dst[r0 + 128:r0 + 256, sl])
        elif s < 32:
            st, sb = C.st16.next()
            P.op("act", lambda e: e.activation(out=st[:], in_=C.ps[tile][:], func=AF.Copy), reads=[C.psb[tile]], writes=[sb])
            c0 = (s - 16) * 512
            P.dma("sp", v[tile * 128:(tile + 1) * 128, c0:c0 + 512], st[:], reads=[sb], writes=[outb], key=sb)
        else:
            ep_gate(s, tile)

    gemm(C, K=D, w_ap=w_in, slabs=list(range(48)), at_res=hT, at_res_b=hTb, mode_of=mode_of, epilogue=ep)
    P.emit(final_waits=C.st16.b + C.st32.b)
    return nc


def run_l2(x, mod, norm_g, w_in):
    cos, sin = rope_tables()
    tabs = np.stack([col_layout(mod[0, 1]), col_layout(mod[0, 0]), col_layout(norm_g[0, 0])], axis=1)
    in_maps = []
    for i in range(NCORE):
        tok = slice(i * T, (i + 1) * T)
        rope = np.stack([cos[:, tok], sin[:, tok], cos[:, tok] / 16.0, sin[:, tok] / 16.0], axis=1).astype(np.float32)
        in_maps.append({"xT": np.ascontiguousarray(x[tok].T), "tabs": tabs, "rope": np.ascontiguousarray(rope), "w_in": w_in})
    nc = build_l2()
    res = run_bass_kernel_spmd(nc, in_maps, core_ids=list(range(NCORE)))
    return res.results


NCH = SEQ // 128
SUP = 4


def ret_consts():
    j = np.arange(128, dtype=np.float32)[:, None]
    i = np.arange(128, dtype=np.float32)[None, :]
    cst = np.zeros((128, 6, 128), np.float32)
    cst[:, 0] = np.maximum(i - j, 0)
    cst[:, 1] = (i >= j)
    cst[:, 2] = np.maximum(j - i, 0)
    cst[:, 3] = (j >= i)
    cst[:, 4] = np.broadcast_to(i + 1.0, (128, 128))
    cst[:, 5] = np.broadcast_to(128.0 - i, (128, 128))
    col = np.zeros((128, 2), np.float32)
    col[:, 0] = 127.0 - np.arange(128)
    col[:, 1] = np.arange(128)
    ident = np.eye(128, dtype=np.float32).astype(NPBF)
    return cst, col, ident


def build_l3():
    nc = bass.Bass("TRN2", target_bir_lowering=False)
    qT = nc.dram_tensor("qT", [2, DK, SEQ], BF16, kind="ExternalInput").ap()
    kT = nc.dram_tensor("kT", [2, DK, SEQ], BF16, kind="ExternalInput").ap()
    v = nc.dram_tensor("v", [SEQ, 2 * DV], BF16, kind="ExternalInput").ap()
    dec = nc.dram_tensor("dec", [128, 4], F32, kind="ExternalInput").ap()
    cst_d = nc.dram_tensor("cst", [128, 6, 128], F32, kind="ExternalInput").ap()
    col_d = nc.dram_tensor("col", [128, 2], F32, kind="ExternalInput").ap()
    ident_d = nc.dram_tensor("ident", [128, 128], BF16, kind="ExternalInput").ap()
    yT = nc.dram_tensor("yT", [2, DV, SEQ], BF16, kind="ExternalOutput").ap()
    sbd = nc.dram_tensor("sbd", [2, NCH, 128, 2 * DV], BF16).ap()
    P = Prog(nc)
    fw = emit_retention(P, qT, kT, v, dec, cst_d, col_d, ident_d, yT, sbd)
    P.emit(final_waits=fw)
    return nc


def run_l3(qTh, kTh, vh, dec_f, dec_b):
    cst, col, ident = ret_consts()
    in_maps = []
    for i in range(NCORE):
        dec = np.array([dec_f[2 * i], dec_f[2 * i + 1], dec_b[2 * i], dec_b[2 * i + 1]], np.float32)
        in_maps.append({"qT": np.ascontiguousarray(qTh[2 * i:2 * i + 2]), "kT": np.ascontiguousarray(kTh[2 * i:2 * i + 2]),
                        "v": np.ascontiguousarray(vh[:, i * 2 * DV:(i + 1) * 2 * DV]),
                        "dec": np.ascontiguousarray(np.broadcast_to(dec, (128, 4))), "cst": cst, "col": col, "ident": ident})
    nc = build_l3()
    res = run_bass_kernel_spmd(nc, in_maps, core_ids=list(range(NCORE)))
    return np.concatenate([res.results[i]["yT"].reshape(2 * DV, SEQ) for i in range(NCORE)], axis=0)


import os
RET_STAGE = int(os.environ.get("RET_STAGE", "9"))


def emit_retention(P, qT, kT, v, dec, cst_d, col_d, ident_d, yT, sbd, in_bufs=(), out_buf=None):
    banks = [P.psum("rb%d" % i, [128, 512], F32) for i in range(8)]
    in_bufs = list(in_bufs)
    eps_t = P.sbuf("r_eps", [128, 1], F32)
    epsb = Buf("r_eps")
    P.op("pool", lambda e: e.memset(eps_t[:], EPS), writes=[epsb])

    def tab(name, shape, dt, src):
        t = P.sbuf(name, shape, dt)
        b = Buf(name)
        P.dma("sp", t[:], src, writes=[b], key=b)
        return t, b
    cst, cstb = tab("r_cst", [128, 6, 128], F32, cst_d)
    col, colb = tab("r_col", [128, 2], F32, col_d)
    ident, identb = tab("r_ident", [128, 128], BF16, ident_d)
    dx, dxb = tab("r_dec", [128, 4], F32, dec)
    sm = {n: P.sbuf("r_" + n, [128, 4], F32) for n in ("ax", "u", "z", "z2", "p", "mn", "lg")}
    smb = Buf("r_small")

    def dv(fn):
        P.op("dve", fn, reads=[smb, dxb], writes=[smb])
    P.op("act", lambda e: e.activation(out=sm["ax"][:], in_=dx[:], func=AF.Abs), reads=[smb, dxb], writes=[smb])
    P.op("act", lambda e: e.activation(out=sm["u"][:], in_=sm["ax"][:], func=AF.Exp, scale=-1.0), reads=[smb], writes=[smb])
    dv(lambda e: e.tensor_scalar(out=sm["z"][:], in0=sm["u"][:], scalar1=2.0, scalar2=None, op0=ALU.add))
    dv(lambda e: e.reciprocal(out=sm["z"][:], in_=sm["z"][:]))
    dv(lambda e: e.tensor_tensor(out=sm["z"][:], in0=sm["z"][:], in1=sm["u"][:], op=ALU.mult))
    dv(lambda e: e.tensor_tensor(out=sm["z2"][:], in0=sm["z"][:], in1=sm["z"][:], op=ALU.mult))
    dv(lambda e: e.tensor_scalar(out=sm["p"][:], in0=sm["z2"][:], scalar1=1.0 / 11.0, scalar2=1.0 / 9.0, op0=ALU.mult, op1=ALU.add))
    for cc in (1.0 / 7.0, 1.0 / 5.0, 1.0 / 3.0, 1.0):
        dv(lambda e: e.tensor_tensor(out=sm["p"][:], in0=sm["p"][:], in1=sm["z2"][:], op=ALU.mult))
        dv(lambda e, cc=cc: e.tensor_scalar(out=sm["p"][:], in0=sm["p"][:], scalar1=cc, scalar2=None, op0=ALU.add))
    dv(lambda e: e.tensor_tensor(out=sm["p"][:], in0=sm["p"][:], in1=sm["z"][:], op=ALU.mult))
    dv(lambda e: e.tensor_scalar(out=sm["mn"][:], in0=dx[:], scalar1=0.0, scalar2=None, op0=ALU.min))
    dv(lambda e: e.scalar_tensor_tensor(out=sm["lg"][:], in0=sm["p"][:], scalar=-2.0, in1=sm["mn"][:], op0=ALU.mult, op1=ALU.add))
    lg = sm["lg"]
    Mt, qdec, kdec, cdec = {}, {}, {}, {}
    tabb = Buf("r_tabs")
    e1 = P.sbuf("r_e1", [128, 128], F32)
    e2 = P.sbuf("r_e2", [128, 128], F32)
    eb = Buf("r_e")
    for hh in range(2):
        Mt[hh] = P.sbuf("r_Mt%d" % hh, [128, 128], F32)
        lf = lg[:, hh:hh + 1]
        lb = lg[:, 2 + hh:3 + hh]
        P.op("act", lambda e, lf=lf: e.activation(out=e1[:], in_=cst[:, 0, :], func=AF.Exp, scale=lf), reads=[smb, cstb], writes=[eb])
        P.op("act", lambda e, lb=lb: e.activation(out=e2[:], in_=cst[:, 2, :], func=AF.Exp, scale=lb), reads=[smb, cstb], writes=[eb])
        P.op("dve", lambda e: e.tensor_tensor(out=e1[:], in0=e1[:], in1=cst[:, 1, :], op=ALU.mult), reads=[eb, cstb], writes=[eb])
        P.op("dve", lambda e: e.tensor_tensor(out=e2[:], in0=e2[:], in1=cst[:, 3, :], op=ALU.mult), reads=[eb, cstb], writes=[eb])
        P.op("dve", lambda e, hh=hh: e.tensor_tensor(out=Mt[hh][:], in0=e1[:], in1=e2[:], op=ALU.add), reads=[eb], writes=[tabb, eb])
        for d, l_ in ((0, lf), (1, lb)):
            qdec[hh, d] = P.sbuf("r_qd%d%d" % (hh, d), [128, 128], F32)
            kdec[hh, d] = P.sbuf("r_kd%d%d" % (hh, d), [128, 1], F32)
            cdec[hh, d] = P.sbuf("r_cd%d%d" % (hh, d), [128, 1], F32)
            P.op("act", lambda e, hh=hh, d=d, l_=l_: e.activation(out=qdec[hh, d][:], in_=cst[:, 4 + d, :], func=AF.Exp, scale=l_),
                 reads=[smb, cstb], writes=[tabb])
            P.op("act", lambda e, hh=hh, d=d, l_=l_: e.activation(out=kdec[hh, d][:], in_=col[:, d:d + 1], func=AF.Exp, scale=l_),
                 reads=[smb, colb], writes=[tabb])
            P.op("act", lambda e, hh=hh, d=d, l_=l_: e.activation(out=cdec[hh, d][:], in_=l_, func=AF.Exp, scale=128.0),
                 reads=[smb], writes=[tabb])
    S = [P.sbuf("r_S%d" % hh, [128, 2, DV], F32) for hh in range(2)]
    Sb = P.bufs("r_S", 2)
    Sbf = [Ring(P, "r_Sbf%d" % hh, 2, [128, 2, DV], BF16) for hh in range(2)]
    kring = [Ring(P, "r_k%d" % hh, 2, [128, 2, SUP * 128], BF16) for hh in range(2)]
    qring = [Ring(P, "r_q%d" % hh, 2, [128, 2, SUP * 128], BF16) for hh in range(2)]
    vring = [Ring(P, "r_v%d" % hh, 2, [128, SUP, DV], BF16) for hh in range(2)]
    sdring = [Ring(P, "r_sd%d" % hh, 2, [128, 2, DV], BF16) for hh in range(2)]
    kbring = [Ring(P, "r_kb%d" % hh, 2, [128, 256], BF16) for hh in range(2)]
    smring = [Ring(P, "r_sm%d" % hh, 2, [128, 128], BF16) for hh in range(2)]
    qfring = [Ring(P, "r_qf%d" % hh, 2, [128, 2, 2, 128], BF16) for hh in range(2)]
    ynring = [Ring(P, "r_yn%d" % hh, 2, [128, DV], BF16) for hh in range(2)]
    ytring = [Ring(P, "r_yt%d" % hh, 2, [128, 4, 128], BF16) for hh in range(2)]
    stt = [P.sbuf("r_st%d" % hh, [128, 10], F32) for hh in range(2)]
    sttb = P.bufs("r_st", 2)
    psu = [[banks[4 * hh + 0], banks[4 * hh + 1]] for hh in range(2)]
    psy = [banks[4 * hh + 2] for hh in range(2)]
    pst = [banks[4 * hh + 3][:, 0:128].bitcast(BF16) for hh in range(2)]
    pss = [banks[4 * hh + 3][:, 128:256] for hh in range(2)]
    pyt = [banks[4 * hh + 3][:, 256:512].bitcast(BF16) for hh in range(2)]
    psub = [P.bufs("psu%d" % hh, 2) for hh in range(2)]
    psyb = P.bufs("psy", 2)
    pstb = P.bufs("bankD", 2)
    pssb = pstb
    pytb = pstb
    sbd_b = [[Buf("sbd%d_%d" % (hh, c)) for c in range(NCH)] for hh in range(2)]

    kv = [kT[hh].rearrange("(dc p) t -> p dc t", p=128) for hh in range(2)]
    qv = [qT[hh].rearrange("(dc p) t -> p dc t", p=128) for hh in range(2)]
    vv = v.rearrange("(c p) e -> p c e", p=128)
    yv = [yT[hh].rearrange("(ec p) t -> p ec t", p=128) for hh in range(2)]

    def zero_state(hh):
        P.op("pool", lambda e: e.memset(S[hh][:], 0.0), writes=[Sb[hh]])
        t, b = Sbf[hh].next()
        P.op("pool", lambda e: e.memset(t[:], 0.0), writes=[b])
        return t, b

    def load_kv(hh, s4, need_q):
        kt, kb_ = kring[hh].next()
        P.dma("sp", kt[:], kv[hh][:, :, s4 * SUP * 128:(s4 + 1) * SUP * 128], reads=in_bufs, writes=[kb_], key=kb_)
        vt, vb = vring[hh].next()
        P.dma("sp", vt[:], vv[:, s4 * SUP:(s4 + 1) * SUP, hh * DV:(hh + 1) * DV], reads=in_bufs, writes=[vb], key=vb)
        q = None
        if need_q:
            qt, qb = qring[hh].next()
            P.dma("sp", qt[:], qv[hh][:, :, s4 * SUP * 128:(s4 + 1) * SUP * 128], reads=in_bufs, writes=[qb], key=qb)
            q = (qt, qb)
        return (kt, kb_), (vt, vb), q

    def u_tr(hh, d, ld, ci, U):
        (kt, kb_), (vt, vb), _ = ld
        for dc in range(2):
            P.op("pe", lambda e, dc=dc: e.transpose(out=pst[hh][:, dc * 128:(dc + 1) * 128], in_=kt[:, dc, ci * 128:(ci + 1) * 128], identity=ident[:]),
                 reads=[kb_, identb], writes=[pstb[hh]])

    def u_kb(hh, d, ld, ci, U):
        kbt, kbb = kbring[hh].next()
        U["kb"] = (kbt, kbb)
        P.op("dve", lambda e: e.tensor_scalar(out=kbt[:], in0=pst[hh], scalar1=kdec[hh, d][:, 0:1], scalar2=None, op0=ALU.mult),
             reads=[pstb[hh], tabb], writes=[kbb])

    def u_mm(hh, d, ld, ci, U):
        (kt, kb_), (vt, vb), _ = ld
        kbt, kbb = U["kb"]
        for dc in range(2):
            P.op("pe", lambda e, dc=dc: e.matmul(psu[hh][dc][:], lhsT=kbt[:, dc * 128:(dc + 1) * 128], rhs=vt[:, ci, :], start=True, stop=True),
                 reads=[kbb, vb], writes=[psub[hh][dc]])

    def u_stt(hh, d, ld, ci, U):
        for dc in range(2):
            P.op("dve", lambda e, dc=dc: e.scalar_tensor_tensor(out=S[hh][:, dc, :], in0=S[hh][:, dc, :], scalar=cdec[hh, d][:, 0:1],
                                                               in1=psu[hh][dc][:], op0=ALU.mult, op1=ALU.add),
                 reads=[Sb[hh], psub[hh][dc], tabb], writes=[Sb[hh]])

    def u_cast(hh, d, ld, ci, U):
        nt, nb = Sbf[hh].next()
        P.op("act", lambda e: e.activation(out=nt[:], in_=S[hh][:], func=AF.Copy), reads=[Sb[hh]], writes=[nb])
        U["new"] = (nt, nb)

    if RET_STAGE < 1:
        return []
    cur = [zero_state(hh) for hh in range(2)]
    for s4 in range(NCH // SUP - 1, -1, -1):
        ld = [load_kv(hh, s4, False) for hh in range(2)]
        for ci in range(SUP - 1, -1, -1):
            c = s4 * SUP + ci
            for hh in range(2):
                if c < NCH - 1 and RET_STAGE >= 2:
                    P.dma("sp", sbd[hh, c], cur[hh][0][:].rearrange("p a b -> p (a b)"), reads=[cur[hh][1]], writes=[sbd_b[hh][c]], key=cur[hh][1])
            if c > 0:
                U = [dict(), dict()]
                for fn in (u_tr, u_kb, u_mm, u_stt, u_cast):
                    for hh in range(2):
                        fn(hh, 1, ld[hh], ci, U[hh])
                cur = [U[hh]["new"] for hh in range(2)]
    def fwd_chunk(c, ci, ld, cur):
        tsl = slice(ci * 128, (ci + 1) * 128)
        F = [dict(), dict()]
        U = [dict(), dict()]

        def f_load(hh):
            F[hh]["sd"] = None
            if c < NCH - 1:
                sdt, sdb = sdring[hh].next()
                P.dma("sp", sdt[:].rearrange("p a b -> p (a b)"), sbd[hh, c], reads=[sbd_b[hh][c]], writes=[sdb], key=sdb)
                F[hh]["sd"] = (sdt, sdb)

        def f_scores(hh):
            (kt, kb_), (vt, vb), (qt, qb) = ld[hh]
            for dc in range(2):
                P.op("pe", lambda e, dc=dc: e.matmul(pss[hh], lhsT=kt[:, dc, tsl], rhs=qt[:, dc, tsl], start=(dc == 0), stop=(dc == 1)),
                     reads=[kb_, qb], writes=[pssb[hh]])

        def f_sm(hh):
            (kt, kb_), (vt, vb), (qt, qb) = ld[hh]
            smt, smb_ = smring[hh].next()
            P.op("dve", lambda e: e.tensor_tensor(out=smt[:], in0=pss[hh], in1=Mt[hh][:], op=ALU.mult),
                 reads=[pssb[hh], tabb], writes=[smb_])
            qft, qfb = qfring[hh].next()
            for d in range(2):
                for dc in range(2):
                    P.op("pool", lambda e, d=d, dc=dc: e.tensor_tensor(out=qft[:, d, dc, :], in0=qt[:, dc, tsl], in1=qdec[hh, d][:], op=ALU.mult),
                         reads=[qb, tabb], writes=[qfb])
            F[hh]["sm"] = (smt, smb_)
            F[hh]["qf"] = (qft, qfb)

        def f_y(hh):
            (kt, kb_), (vt, vb), (qt, qb) = ld[hh]
            smt, smb_ = F[hh]["sm"]
            qft, qfb = F[hh]["qf"]
            sd = F[hh]["sd"]
            mms = [(smt[:], vt[:, ci, :], [smb_, vb])]
            if c > 0:
                for dc in range(2):
                    mms.append((qft[:, 0, dc, :], cur[hh][0][:, dc, :], [qfb, cur[hh][1]]))
            if sd is not None:
                for dc in range(2):
                    mms.append((qft[:, 1, dc, :], sd[0][:, dc, :], [qfb, sd[1]]))
            n = len(mms)
            for i, (l_ap, r_, rd) in enumerate(mms):
                P.op("pe", lambda e, l_ap=l_ap, r_=r_, i=i: e.matmul(psy[hh][:], lhsT=l_ap, rhs=r_, start=(i == 0), stop=(i == n - 1)),
                     reads=rd, writes=[psyb[hh]])

        def f_stats1(hh):
            P.op("dve", lambda e: e.bn_stats(out=stt[hh][:, 0:6], in_=psy[hh][:]), reads=[psyb[hh]], writes=[sttb[hh]])
            P.op("dve", lambda e: e.bn_aggr(out=stt[hh][:, 6:8], in_=stt[hh][:, 0:6]), reads=[sttb[hh]], writes=[sttb[hh]])

        def f_sqrt(hh):
            P.op("act", lambda e: e.activation(out=stt[hh][:, 7:8], in_=stt[hh][:, 7:8], func=AF.Sqrt, bias=eps_t[:, 0:1]),
                 reads=[sttb[hh], epsb], writes=[sttb[hh]])

        def f_rs(hh):
            P.op("dve", lambda e: e.reciprocal(out=stt[hh][:, 7:8], in_=stt[hh][:, 7:8]), reads=[sttb[hh]], writes=[sttb[hh]])
            P.op("dve", lambda e: e.scalar_tensor_tensor(out=stt[hh][:, 8:9], in0=stt[hh][:, 6:7], scalar=-1.0, in1=stt[hh][:, 7:8],
                                                        op0=ALU.mult, op1=ALU.mult),
                 reads=[sttb[hh]], writes=[sttb[hh]])

        def f_yn(hh):
            ynt, ynb = ynring[hh].next()
            F[hh]["yn"] = (ynt, ynb)
            P.op("act", lambda e: e.activation(out=ynt[:], in_=psy[hh][:], func=AF.Identity, bias=stt[hh][:, 8:9], scale=stt[hh][:, 7:8]),
                 reads=[psyb[hh], sttb[hh]], writes=[ynb])

        def f_tr(hh):
            ynt, ynb = F[hh]["yn"]
            for ec in range(4):
                P.op("pe", lambda e, ec=ec: e.transpose(out=pyt[hh][:, ec * 128:(ec + 1) * 128], in_=ynt[:, ec * 128:(ec + 1) * 128], identity=ident[:]),
                     reads=[ynb, identb], writes=[pytb[hh]])

        def f_out(hh):
            ytt, ytb = ytring[hh].next()
            P.op("act", lambda e: e.activation(out=ytt[:].rearrange("p a b -> p (a b)"), in_=pyt[hh], func=AF.Copy),
                 reads=[pytb[hh]], writes=[ytb])
            P.dma("sp", yv[hh][:, :, c * 128:(c + 1) * 128], ytt[:], reads=[ytb], writes=[out_buf] if out_buf is not None else [], key=ytb)

        def both(fn, *a):
            for hh in range(2):
                fn(hh, *a)
        upd = c < NCH - 1 and RET_STAGE >= 6
        both(f_load)
        both(f_scores)
        both(f_sm)
        both(f_y)
        if upd:
            for fn in (u_tr, u_kb, u_mm):
                for hh in range(2):
                    fn(hh, 0, ld[hh], ci, U[hh])
        both(f_stats1)
        both(f_sqrt)
        both(f_rs)
        if upd:
            for fn in (u_stt, u_cast):
                for hh in range(2):
                    fn(hh, 0, ld[hh], ci, U[hh])
        both(f_yn)
        both(f_tr)
        both(f_out)
        if upd:
            cur = [U[hh]["new"] for hh in range(2)]
        return cur

    if RET_STAGE < 3:
        return []
    cur = [zero_state(hh) for hh in range(2)]
    for s4 in range(NCH // SUP):
        ld = [load_kv(hh, s4, True) for hh in range(2)]
        for ci_ in range(SUP):
            cur = fwd_chunk(s4 * SUP + ci_, ci_, ld, cur)

    return [b for hh in range(2) for b in ytring[hh].b]


def tab_prod(C, tb, tbb, name, ia, ib, plus1):
    o = C.P.sbuf(name, [128, KC], F32)
    if plus1:
        C.P.op("dve", lambda e: e.scalar_tensor_tensor(out=o[:], in0=tb[:, ia, :], scalar=1.0, in1=tb[:, ib, :], op0=ALU.add, op1=ALU.mult),
               reads=[tbb], writes=[tbb])
    else:
        C.P.op("dve", lambda e: e.tensor_tensor(out=o[:], in0=tb[:, ia, :], in1=tb[:, ib, :], op=ALU.mult), reads=[tbb], writes=[tbb])
    return o


def fm(_s):
    return "fm"


def mlp_block(C, x_ap, x_b, xo_ap, xo_b, rstd, rstd_b, wcol, bcol, ggcol, tbb, hT, hTb, w1, w2, aT, aT_b, yo, yo_b, nstats):
    phase_a(C, x_ap, x_b, rstd, rstd_b, wcol, bcol, tbb, hT, hTb, do_stats=False)
    gemm(C, K=D, w_ap=w1, slabs=list(range(DFF // 512)), at_res=hT, at_res_b=hTb, mode_of=fm,
         epilogue=ep_act_bf16(C, aT, aT_b, AF.Relu, square_after=True))
    gemm(C, K=DFF, w_ap=w2, slabs=list(range(D // 512)), at_dram=aT, at_dram_b=aT_b, mode_of=fm,
         epilogue=ep_store_f32(C, yo, yo_b), kbs=8)
    phase_c(C, yo, yo_b, x_ap, x_b, xo_ap, xo_b, ggcol, tbb, rstd, rstd_b, nstats=nstats, from_acc=True)


def dft_consts():
    f = np.arange(512)
    ang = 2 * np.pi * np.outer(f, f) / 512.0
    cs = np.concatenate([np.cos(ang), -np.sin(ang)], axis=1).astype(np.float32)
    cs = cs.reshape(4, 128, 1024).transpose(1, 0, 2)
    return np.ascontiguousarray(cs).astype(NPBF)


def build_l4():
    nc = bass.Bass("TRN2", target_bir_lowering=False)
    xT = nc.dram_tensor("xT", [D, T], F32, kind="ExternalInput").ap()
    yT = nc.dram_tensor("yT", [VW, T], BF16, kind="ExternalInput").ap()
    sgT = nc.dram_tensor("sgT", [VW, T], BF16, kind="ExternalInput").ap()
    gng = nc.dram_tensor("gng", [128, VW // 128], F32, kind="ExternalInput").ap()
    tabs = nc.dram_tensor("tabs", [128, 10, KC], F32, kind="ExternalInput").ap()
    w_out = nc.dram_tensor("w_out", [VW, D], F32, kind="ExternalInput").ap()
    w1 = nc.dram_tensor("w1", [D, DFF], F32, kind="ExternalInput").ap()
    w2 = nc.dram_tensor("w2", [DFF, D], F32, kind="ExternalInput").ap()
    cs_d = nc.dram_tensor("cs", [128, 4, 1024], BF16, kind="ExternalInput").ap()
    x1T = nc.dram_tensor("x1T", [D, T], F32, kind="ExternalOutput").ap()
    Z = nc.dram_tensor("Z", [8, 2, T, 512], BF16, kind="ExternalOutput").ap()
    ygT = nc.dram_tensor("ygT", [VW, T], BF16).ap()
    yo = nc.dram_tensor("yo", [D, T], F32).ap()
    xs = nc.dram_tensor("xs", [D, T], F32).ap()
    aT = nc.dram_tensor("aT", [DFF, T], BF16).ap()
    P = Prog(nc)
    C = Ctx(P)
    setup_eps(C)
    tb, tbb = load_tab(C, "tabs", tabs, [128, 10, KC])
    gn, gnb = load_tab(C, "gng", gng, [128, VW // 128])
    cs = P.sbuf("cs_sb", [128, 4, 1024], BF16)
    csb = Buf("cs")
    P.dma("sp", cs[:], cs_d, writes=[csb], key=csb)
    gg1 = tab_prod(C, tb, tbb, "gg1", 0, 1, False)
    wc2 = tab_prod(C, tb, tbb, "wc2", 2, 4, True)
    gg2 = tab_prod(C, tb, tbb, "gg2", 5, 6, False)
    wc1n = tab_prod(C, tb, tbb, "wc1n", 7, 9, True)
    hT = P.sbuf("hT", [128, KC, T], BF16)
    hTb = P.bufs("hT", KC)
    rstd = P.sbuf("rstd", [128, T], F32)
    rstdb = Buf("rstd")
    yg_b, yo_b, xs_b, aT_b, x1_b, z_b = Buf("ygT"), Buf("yo"), Buf("xs"), Buf("aT"), Buf("x1T"), Buf("Z")
    yv = yT.rearrange("(kc p) t -> kc p t", p=128)
    sv = sgT.rearrange("(kc p) t -> kc p t", p=128)
    gv = ygT.rearrange("(kc p) t -> kc p t", p=128)
    for kc in range(VW // 128):
        yt, yb = load_tile(C, yv[kc], None, BF16)
        st, sb = load_tile(C, sv[kc], None, BF16)
        P.op("dve", lambda e, yt=yt, st=st, kc=kc: e.scalar_tensor_tensor(out=yt[:], in0=yt[:], scalar=gn[:, kc:kc + 1], in1=st[:], op0=ALU.mult, op1=ALU.mult),
             reads=[yb, sb, gnb], writes=[yb])
        P.dma("sp", gv[kc], yt[:], reads=[yb], writes=[yg_b], key=yb)
    gemm(C, K=VW, w_ap=w_out, slabs=list(range(D // 512)), at_dram=ygT, at_dram_b=yg_b, mode_of=fm,
         epilogue=ep_store_f32(C, yo, yo_b), kbs=8)
    phase_c(C, yo, yo_b, xT, None, xs, xs_b, gg1, tbb, rstd, rstdb, from_acc=True)
    mlp_block(C, xs, xs_b, x1T, x1_b, rstd, rstdb, wc2, tb[:, 3, :], gg2, tbb, hT, hTb, w1, w2, aT, aT_b, yo, yo_b, True)
    phase_a(C, x1T, x1_b, rstd, rstdb, wc1n, tb[:, 8, :], tbb, hT, hTb, do_stats=False)
    n = 0
    for g in range(8):
        for tt in range(8):
            for ri in range(2):
                bank = n % 8
                n += 1
                for kc in range(4):
                    P.op("pe", lambda e, bank=bank, g=g, kc=kc, tt=tt, ri=ri: e.matmul(C.ps[bank][:], lhsT=hT[:, g * 4 + kc, tt * 128:(tt + 1) * 128],
                                                                                    rhs=cs[:, kc, ri * 512:(ri + 1) * 512], start=(kc == 0), stop=(kc == 3)),
                         reads=[hTb[g * 4 + kc], csb], writes=[C.psb[bank]])
                st, sb = C.st16.next()
                if n % 2 == 0:
                    P.op("act", lambda e, st=st, bank=bank: e.activation(out=st[:], in_=C.ps[bank][:], func=AF.Copy), reads=[C.psb[bank]], writes=[sb])
                else:
                    P.op("dve", lambda e, st=st, bank=bank: e.tensor_copy(out=st[:], in_=C.ps[bank][:]), reads=[C.psb[bank]], writes=[sb])
                P.dma("sp", Z[g, ri, tt * 128:(tt + 1) * 128, :], st[:], reads=[sb], writes=[z_b], key=sb)
    P.emit(final_waits=C.st16.b + C.st32.b + C.f32r.b + C.bf1k.b)
    return nc


def run_l4(x, yT_all, sg_parts, mod, norm_g, gn_g, w_out, w1, w2):
    tabs = np.stack([col_layout(mod[0, 2]), col_layout(norm_g[0, 1]), col_layout(mod[0, 4]), col_layout(mod[0, 3]),
                     col_layout(norm_g[0, 2]), col_layout(mod[0, 5]), col_layout(norm_g[0, 3]),
                     col_layout(mod[1, 1]), col_layout(mod[1, 0]), col_layout(norm_g[1, 0])], axis=1)
    gng = col_layout(gn_g)
    cs = dft_consts()
    in_maps = []
    for i in range(NCORE):
        tok = slice(i * T, (i + 1) * T)
        in_maps.append({"xT": np.ascontiguousarray(x[tok].T), "yT": np.ascontiguousarray(yT_all[:, tok]), "sgT": sg_parts[i],
                        "gng": gng, "tabs": np.ascontiguousarray(tabs), "w_out": w_out, "w1": w1, "w2": w2, "cs": cs})
    nc = build_l4()
    res = run_bass_kernel_spmd(nc, in_maps, core_ids=list(range(NCORE)))
    return res.results


def fft_consts():
    a = np.arange(128)
    ang = 2 * np.pi * np.outer(a, a) / 128.0
    f128 = np.stack([np.cos(ang), np.sin(ang), -np.sin(ang)], axis=1).astype(np.float32).astype(NPBF)
    n1 = np.arange(128)[:, None]
    m2 = np.arange(64)[None, :]
    th = 2 * np.pi * (n1 * m2) / 8192.0
    tw = np.stack([np.cos(th), np.sin(th)], axis=1).astype(np.float32)
    b = np.arange(64)
    ang64 = 2 * np.pi * np.outer(b, b) / 64.0
    f64 = np.stack([np.cos(ang64), np.sin(ang64)], axis=1).astype(np.float32).astype(NPBF)
    return np.ascontiguousarray(f128), np.ascontiguousarray(tw), np.ascontiguousarray(f64)


def build_l5():
    nc = bass.Bass("TRN2", target_bir_lowering=False)
    Zr = nc.dram_tensor("Zr", [SEQ, 512], BF16, kind="ExternalInput").ap()
    Zi = nc.dram_tensor("Zi", [SEQ, 512], BF16, kind="ExternalInput").ap()
    f128_d = nc.dram_tensor("f128", [128, 3, 128], BF16, kind="ExternalInput").ap()
    tw_d = nc.dram_tensor("tw", [128, 2, 64], F32, kind="ExternalInput").ap()
    f64_d = nc.dram_tensor("f64", [64, 2, 64], BF16, kind="ExternalInput").ap()
    mixT = nc.dram_tensor("mixT", [512, SEQ], BF16, kind="ExternalOutput").ap()
    Yp = nc.dram_tensor("Yp", [2, 128, 64, 512], BF16).ap()
    P = Prog(nc)
    fw = emit_fft(P, Zr, Zi, f128_d, tw_d, f64_d, mixT, Yp)
    P.emit(final_waits=fw)
    return nc


def emit_fft(P, Zr, Zi, f128_d, tw_d, f64_d, mixT, Yp, in_bufs=(), out_buf=None):
    in_bufs = list(in_bufs)
    banks = [P.psum("fb%d" % i, [128, 512], F32) for i in range(8)]
    bb = P.bufs("fb", 8)

    def tab(name, shape, dt, src):
        t = P.sbuf(name, shape, dt)
        b = Buf(name)
        P.dma("sp", t[:], src, writes=[b], key=b)
        return t, b
    f128, f128b = tab("f_f128", [128, 3, 128], BF16, f128_d)
    tw, twb = tab("f_tw", [128, 2, 64], F32, tw_d)
    f64, f64b = tab("f_f64", [64, 2, 64], BF16, f64_d)
    G = 8
    xr_ring = Ring(P, "f_xr", 2, [128, G, 512], BF16)
    xi_ring = Ring(P, "f_xi", 2, [128, G, 512], BF16)
    yr_ring = Ring(P, "f_yr", 2, [128, G, 512], BF16)
    yi_ring = Ring(P, "f_yi", 2, [128, G, 512], BF16)
    t_ring = Ring(P, "f_t", 4, [128, 512], F32)
    zr_v = Zr.rearrange("(m1 m2) f -> m1 m2 f", m2=64)
    zi_v = Zi.rearrange("(m1 m2) f -> m1 m2 f", m2=64)
    yp_b = Buf("Yp")
    n = 0
    for gb in range(64 // G):
        xr, xrb = xr_ring.next()
        xi, xib = xi_ring.next()
        P.dma("sp", xr[:], zr_v[:, gb * G:(gb + 1) * G, :], reads=in_bufs, writes=[xrb], key=xrb)
        P.dma("sp", xi[:], zi_v[:, gb * G:(gb + 1) * G, :], reads=in_bufs, writes=[xib], key=xib)
        yr, yrb = yr_ring.next()
        yi, yib = yi_ring.next()
        for j in range(G):
            m2 = gb * G + j
            bA, bB = (n % 4) * 2, (n % 4) * 2 + 1
            n += 1
            P.op("pe", lambda e, bA=bA, xr=xr, j=j: e.matmul(banks[bA][:], lhsT=f128[:, 0, :], rhs=xr[:, j, :], start=True, stop=False), reads=[f128b, xrb], writes=[bb[bA]])
            P.op("pe", lambda e, bA=bA, xi=xi, j=j: e.matmul(banks[bA][:], lhsT=f128[:, 1, :], rhs=xi[:, j, :], start=False, stop=True), reads=[f128b, xib], writes=[bb[bA]])
            P.op("pe", lambda e, bB=bB, xi=xi, j=j: e.matmul(banks[bB][:], lhsT=f128[:, 0, :], rhs=xi[:, j, :], start=True, stop=False), reads=[f128b, xib], writes=[bb[bB]])
            P.op("pe", lambda e, bB=bB, xr=xr, j=j: e.matmul(banks[bB][:], lhsT=f128[:, 2, :], rhs=xr[:, j, :], start=False, stop=True), reads=[f128b, xrb], writes=[bb[bB]])
            c_ = tw[:, 0, m2:m2 + 1]
            s_ = tw[:, 1, m2:m2 + 1]
            t1, t1b = t_ring.next()
            t2, t2b = t_ring.next()
            P.op("act", lambda e, t1=t1, bB=bB, s_=s_: e.activation(out=t1[:], in_=banks[bB][:], func=AF.Copy, scale=s_), reads=[bb[bB], twb], writes=[t1b])
            P.op("act", lambda e, t2=t2, bA=bA, s_=s_: e.activation(out=t2[:], in_=banks[bA][:], func=AF.Copy, scale=s_), reads=[bb[bA], twb], writes=[t2b])
            P.op("dve", lambda e, yr=yr, j=j, bA=bA, c_=c_, t1=t1: e.scalar_tensor_tensor(out=yr[:, j, :], in0=banks[bA][:], scalar=c_, in1=t1[:], op0=ALU.mult, op1=ALU.add),
                 reads=[bb[bA], twb, t1b, t2b], writes=[yrb])
            P.op("dve", lambda e, yi=yi, j=j, bB=bB, c_=c_, t2=t2: e.scalar_tensor_tensor(out=yi[:, j, :], in0=banks[bB][:], scalar=c_, in1=t2[:], op0=ALU.mult, op1=ALU.subtract),
                 reads=[bb[bB], twb, t2b], writes=[yib])
        P.dma("sp", Yp[0, :, gb * G:(gb + 1) * G, :], yr[:], reads=[yrb], writes=[yp_b], key=yrb)
        P.dma("sp", Yp[1, :, gb * G:(gb + 1) * G, :], yi[:], reads=[yib], writes=[yp_b], key=yib)
    NB = 16
    ar_ring = Ring(P, "f_ar", 2, [64, NB, 512], BF16)
    ai_ring = Ring(P, "f_ai", 2, [64, NB, 512], BF16)
    mt = [P.sbuf("f_mt%d" % fc, [128, 64, 128], BF16) for fc in range(4)]
    mtb = P.bufs("f_mt", 4)
    ypr = Yp[0].rearrange("n1 m2 f -> m2 n1 f")
    ypi = Yp[1].rearrange("n1 m2 f -> m2 n1 f")
    for nb in range(128 // NB):
        ar, arb = ar_ring.next()
        ai, aib = ai_ring.next()
        P.dma("sp", ar[:], ypr[:, nb * NB:(nb + 1) * NB, :], reads=[yp_b], writes=[arb], key=arb)
        P.dma("sp", ai[:], ypi[:, nb * NB:(nb + 1) * NB, :], reads=[yp_b], writes=[aib], key=aib)
        for half in range(NB // 8):
            for fc in range(4):
                bank = (half % 2) * 4 + fc
                for j in range(8):
                    jj = half * 8 + j
                    P.op("pe", lambda e, bank=bank, ar=ar, jj=jj, fc=fc, j=j: e.matmul(banks[bank][:, j * 64:(j + 1) * 64], lhsT=ar[:, jj, fc * 128:(fc + 1) * 128],
                                                                                  rhs=f64[:, 0, :], start=True, stop=False),
                         reads=[arb, f64b], writes=[bb[bank]])
                    P.op("pe", lambda e, bank=bank, ai=ai, jj=jj, fc=fc, j=j: e.matmul(banks[bank][:, j * 64:(j + 1) * 64], lhsT=ai[:, jj, fc * 128:(fc + 1) * 128],
                                                                                  rhs=f64[:, 1, :], start=False, stop=True),
                         reads=[aib, f64b], writes=[bb[bank]])
                n1b = nb * NB + half * 8
                src = banks[bank][:].rearrange("p (j n2) -> p j n2", j=8)
                dst = mt[fc][:, :, n1b:n1b + 8].rearrange("p n2 j -> p j n2")
                if fc % 2 == 0:
                    P.op("act", lambda e, dst=dst, src=src: e.activation(out=dst, in_=src, func=AF.Copy), reads=[bb[bank]], writes=[mtb[fc]])
                else:
                    P.op("dve", lambda e, dst=dst, src=src: e.tensor_copy(out=dst, in_=src), reads=[bb[bank]], writes=[mtb[fc]])
    for fc in range(4):
        P.dma("sp", mixT[fc * 128:(fc + 1) * 128, :], mt[fc][:].rearrange("p a b -> p (a b)"), reads=[mtb[fc]],
              writes=[out_buf] if out_buf is not None else [], key=mtb[fc])
    return list(mtb)


def run_l5(Zr_g, Zi_g):
    f128, tw, f64 = fft_consts()
    in_maps = [{"Zr": np.ascontiguousarray(Zr_g[i]), "Zi": np.ascontiguousarray(Zi_g[i]), "f128": f128, "tw": tw, "f64": f64} for i in range(NCORE)]
    nc = build_l5()
    res = run_bass_kernel_spmd(nc, in_maps, core_ids=list(range(NCORE)))
    return np.concatenate([res.results[i]["mixT"] for i in range(NCORE)], axis=0)


def build_l6():
    nc = bass.Bass("TRN2", target_bir_lowering=False)
    x1T = nc.dram_tensor("x1T", [D, T], F32, kind="ExternalInput").ap()
    mixT = nc.dram_tensor("mixT", [D, T], BF16, kind="ExternalInput").ap()
    tabs = nc.dram_tensor("tabs", [128, 8, KC], F32, kind="ExternalInput").ap()
    fno_w = nc.dram_tensor("fno_w", [D, D], F32, kind="ExternalInput").ap()
    w1 = nc.dram_tensor("w1", [D, DFF], F32, kind="ExternalInput").ap()
    w2 = nc.dram_tensor("w2", [DFF, D], F32, kind="ExternalInput").ap()
    xoT = nc.dram_tensor("xoT", [D, T], F32, kind="ExternalOutput").ap()
    yo = nc.dram_tensor("yo", [D, T], F32).ap()
    xs = nc.dram_tensor("xs", [D, T], F32).ap()
    aT = nc.dram_tensor("aT", [DFF, T], BF16).ap()
    P = Prog(nc)
    C = Ctx(P)
    setup_eps(C)
    tb, tbb = load_tab(C, "tabs", tabs, [128, 8, KC])
    gg1 = tab_prod(C, tb, tbb, "gg1", 0, 1, False)
    wc2 = tab_prod(C, tb, tbb, "wc2", 2, 4, True)
    gg2 = tab_prod(C, tb, tbb, "gg2", 5, 6, False)
    hT = P.sbuf("hT", [128, KC, T], BF16)
    hTb = P.bufs("hT", KC)
    rstd = P.sbuf("rstd", [128, T], F32)
    rstdb = Buf("rstd")
    yo_b, xs_b, aT_b, xo_b = Buf("yo"), Buf("xs"), Buf("aT"), Buf("xoT")
    mv = mixT.rearrange("(kc p) t -> kc p t", p=128)
    for kc in range(KC):
        P.dma("sp", hT[:, kc, :], mv[kc], writes=[hTb[kc]], key=hTb[kc])
    gemm(C, K=D, w_ap=fno_w, slabs=list(range(D // 512)), at_res=hT, at_res_b=hTb, mode_of=fm,
         epilogue=ep_store_f32(C, yo, yo_b, fbcol=tb[:, 7, :], tab_b=tbb))
    phase_c(C, yo, yo_b, x1T, None, xs, xs_b, gg1, tbb, rstd, rstdb, fbcol=tb[:, 7, :], from_acc=True)
    mlp_block(C, xs, xs_b, xoT, xo_b, rstd, rstdb, wc2, tb[:, 3, :], gg2, tbb, hT, hTb, w1, w2, aT, aT_b, yo, yo_b, False)
    P.emit(final_waits=C.st16.b + C.st32.b + C.f32r.b + C.bf1k.b)
    return nc


def run_l6(x1T_parts, mixT_all, mod, norm_g, fno_b, fno_w, w1, w2):
    tabs = np.stack([col_layout(mod[1, 2]), col_layout(norm_g[1, 1]), col_layout(mod[1, 4]), col_layout(mod[1, 3]),
                     col_layout(norm_g[1, 2]), col_layout(mod[1, 5]), col_layout(norm_g[1, 3]), col_layout(fno_b)], axis=1)
    in_maps = []
    for i in range(NCORE):
        tok = slice(i * T, (i + 1) * T)
        in_maps.append({"x1T": x1T_parts[i], "mixT": np.ascontiguousarray(mixT_all[:, tok]), "tabs": np.ascontiguousarray(tabs),
                        "fno_w": fno_w, "w1": w1, "w2": w2})
    nc = build_l6()
    res = run_bass_kernel_spmd(nc, in_maps, core_ids=list(range(NCORE)))
    return res.results


def kernel(x, c, ada_w, ada_b, norm_g, ret_w_in, ret_w_out, ret_gn_g, ret_decay_fwd, ret_decay_bwd,
           fno_w, fno_b, mlp_w1, mlp_w2):
    x = np.asarray(x, np.float32)[0]
    mod = run_l1(np.asarray(c, np.float32)[0], np.asarray(ada_w, np.float32), np.asarray(ada_b, np.float32))
    norm_g = np.asarray(norm_g, np.float32)
    r2 = run_l2(x, mod, norm_g, np.asarray(ret_w_in, np.float32)[0])
    qT = np.concatenate([r2[i]["qT"] for i in range(NCORE)], axis=1).reshape(HEADS, DK, SEQ)
    kT = np.concatenate([r2[i]["kT"] for i in range(NCORE)], axis=1).reshape(HEADS, DK, SEQ)
    v = np.concatenate([r2[i]["v"] for i in range(NCORE)], axis=0)
    sg = [r2[i]["sgT"] for i in range(NCORE)]
    del r2
    yT = run_l3(qT, kT, v, np.asarray(ret_decay_fwd, np.float32)[0], np.asarray(ret_decay_bwd, np.float32)[0])
    del qT, kT, v
    r4 = run_l4(x, yT, sg, mod, norm_g, np.asarray(ret_gn_g, np.float32)[0], np.asarray(ret_w_out, np.float32)[0],
                np.asarray(mlp_w1, np.float32)[0], np.asarray(mlp_w2, np.float32)[0])
    Z = np.concatenate([r4[i]["Z"] for i in range(NCORE)], axis=2)
    x1 = [r4[i]["x1T"] for i in range(NCORE)]
    del r4
    mixT = run_l5(Z[:, 0], Z[:, 1])
    r6 = run_l6(x1, mixT, mod, norm_g, np.asarray(fno_b, np.float32)[0], np.asarray(fno_w, np.float32)[0],
                np.asarray(mlp_w1, np.float32)[1], np.asarray(mlp_w2, np.float32)[1])
    out = np.concatenate([r6[i]["xoT"].T for i in range(NCORE)], axis=0)
    return np.ascontiguousarray(out[None]).astype(np.float32)
```
